# Optimizing a Trainium2 kernel written in Bass

```python
import math
import jax
import jax.numpy as jnp
from jax import lax
import numpy as np

D_MODEL = 1024
BATCH = 32
SEQ = 2048
DEPTH = 4

GRID_W = 64
CTX_LEN = 256

N_MIXERS = 3
N_A_LAYERS = (DEPTH + 2) // N_MIXERS
N_B_LAYERS = (DEPTH + 1) // N_MIXERS
N_C_LAYERS = DEPTH // N_MIXERS

BLOCK = 128
WINDOW = 128
BAND = BLOCK + 2 * WINDOW

A_HEAD_DIM = 64
A_HEADS = D_MODEL // A_HEAD_DIM
A_KV_HEADS = A_HEADS // 4
A_GROUP = A_HEADS // A_KV_HEADS
A_Q_DIM = A_HEADS * A_HEAD_DIM
A_KV_DIM = A_KV_HEADS * A_HEAD_DIM
A_QKV_DIM = A_Q_DIM + 2 * A_KV_DIM

CHUNK = 128
B_WIDTH = D_MODEL
B_GROUPS = 8
B_GROUP_W = B_WIDTH // B_GROUPS

C_HEAD_DIM = 128
C_HEADS = D_MODEL // C_HEAD_DIM
C_KV_HEADS = C_HEADS // 2
C_GROUP = C_HEADS // C_KV_HEADS
C_Q_DIM = C_HEADS * C_HEAD_DIM
C_KV_DIM = C_KV_HEADS * C_HEAD_DIM
C_QKV_DIM = C_Q_DIM + 2 * C_KV_DIM

FFN_HIDDEN = int(math.ceil(8 * D_MODEL / 3 / 256)) * 256

ROPE_THETA = 10000.0
RMS_EPS = 1e-6
LN_EPS = 1e-5
NEG_INF = -1e30

kernel_name = 'hybrid_interleaved_dit_prefix_ctx'


def rmsnorm(x, g):
    xf = x.astype(jnp.float32)
    y = xf * lax.rsqrt(jnp.mean(xf * xf, axis=-1, keepdims=True) + RMS_EPS)
    return (y * g.astype(jnp.float32)).astype(x.dtype)


def layernorm(x, g, b):
    xf = x.astype(jnp.float32)
    mu = jnp.mean(xf, axis=-1, keepdims=True)
    var = jnp.mean(jnp.square(xf - mu), axis=-1, keepdims=True)
    y = (xf - mu) * lax.rsqrt(var + LN_EPS)
    return (y * g.astype(jnp.float32) + b.astype(jnp.float32)).astype(x.dtype)


def modulate(h, shift, scale):
    return h * (1 + scale) + shift


def axial_rope_tables(n_tokens, head_dim):
    rows = n_tokens // GRID_W
    row_pos = jnp.repeat(jnp.arange(rows, dtype=jnp.float32), GRID_W)
    col_pos = jnp.tile(jnp.arange(GRID_W, dtype=jnp.float32), rows)
    n_freq = head_dim // 4
    inv_freq = ROPE_THETA ** (-jnp.arange(n_freq, dtype=jnp.float32) / n_freq)
    angles = jnp.concatenate([row_pos[:, None] * inv_freq, col_pos[:, None] * inv_freq], axis=-1)
    return jnp.cos(angles), jnp.sin(angles)


def apply_rope(x, cos, sin):
    xf = x.astype(jnp.float32)
    x1, x2 = jnp.split(xf, 2, axis=-1)
    c = cos[None, :, None, :]
    s = sin[None, :, None, :]
    return jnp.concatenate([x1 * c - x2 * s, x2 * c + x1 * s], axis=-1).astype(x.dtype)


def joint_softmax_attend(logit_parts, value_parts, sink=None):
    sizes = [p.shape[-1] for p in logit_parts]
    logits = jnp.concatenate([p.astype(jnp.float32) for p in logit_parts], axis=-1)
    if sink is not None:
        sink_col = jnp.broadcast_to(sink.astype(jnp.float32)[None, :, :, None, None], logits.shape[:-1] + (1,))
        logits = jnp.concatenate([logits, sink_col], axis=-1)
    probs = jax.nn.softmax(logits, axis=-1)
    out = None
    offset = 0
    for size, v in zip(sizes, value_parts):
        p = probs[..., offset:offset + size].astype(v.dtype)
        term = jnp.einsum('bhgqk,bkhd->bqhgd', p, v)
        out = term if out is None else out + term
        offset += size
    return out


def window_sink_mixer(h_lat, h_ctx, w_qkv, w_o, sink, cos, sin, ctx_out):
    bsz, n, _ = h_lat.shape
    n_ctx = h_ctx.shape[1]
    nb = n // BLOCK
    scale = A_HEAD_DIM ** -0.5
    q, k, v = jnp.split(h_lat @ w_qkv, [A_Q_DIM, A_Q_DIM + A_KV_DIM], axis=-1)
    q = apply_rope(q.reshape(bsz, n, A_HEADS, A_HEAD_DIM), cos, sin) * scale
    k = apply_rope(k.reshape(bsz, n, A_KV_HEADS, A_HEAD_DIM), cos, sin)
    v = v.reshape(bsz, n, A_KV_HEADS, A_HEAD_DIM)
    k_c, v_c = jnp.split(h_ctx @ w_qkv[:, A_Q_DIM:], 2, axis=-1)
    k_c = k_c.reshape(bsz, n_ctx, A_KV_HEADS, A_HEAD_DIM)
    v_c = v_c.reshape(bsz, n_ctx, A_KV_HEADS, A_HEAD_DIM)
    sink_g = sink.reshape(A_KV_HEADS, A_GROUP)

    qb = jnp.moveaxis(q.reshape(bsz, nb, BLOCK, A_KV_HEADS, A_GROUP, A_HEAD_DIM), 1, 0)
    pad = ((0, 0), (WINDOW, WINDOW), (0, 0), (0, 0))
    kp = jnp.pad(k, pad)
    vp = jnp.pad(v, pad)

    def attend_block(args):
        q_blk, blk = args
        start = blk * BLOCK
        k_band = lax.dynamic_slice_in_dim(kp, start, BAND, axis=1)
        v_band = lax.dynamic_slice_in_dim(vp, start, BAND, axis=1)
        qpos = start + jnp.arange(BLOCK)
        kpos = start - WINDOW + jnp.arange(BAND)
        mask = (jnp.abs(qpos[:, None] - kpos[None, :]) <= WINDOW) & (kpos[None, :] >= 0) & (kpos[None, :] < n)
        s_band = jnp.einsum('bqhgd,bkhd->bhgqk', q_blk, k_band).astype(jnp.float32)
        s_band = jnp.where(mask, s_band, NEG_INF)
        s_ctx = jnp.einsum('bqhgd,bkhd->bhgqk', q_blk, k_c)
        return joint_softmax_attend([s_band, s_ctx], [v_band, v_c], sink_g)

    o = lax.map(attend_block, (qb, jnp.arange(nb)))
    y_lat = jnp.moveaxis(o, 0, 1).reshape(bsz, n, A_Q_DIM) @ w_o
    y_ctx = None
    if ctx_out:
        q_c = (h_ctx @ w_qkv[:, :A_Q_DIM]).reshape(bsz, n_ctx, A_KV_HEADS, A_GROUP, A_HEAD_DIM) * scale
        s_cc = jnp.einsum('bqhgd,bkhd->bhgqk', q_c, k_c)
        y_ctx = joint_softmax_attend([s_cc], [v_c], sink_g).reshape(bsz, n_ctx, A_Q_DIM) @ w_o
    return y_lat, y_ctx


def chunk_gmlp(h, w_in, b_in, ln_g, ln_b, w_s, b_s, w_o):
    bsz, n, _ = h.shape
    z = jax.nn.gelu(h @ w_in + b_in, approximate=False)
    u, v = jnp.split(z, 2, axis=-1)
    v = layernorm(v, ln_g, ln_b).reshape(bsz, n // CHUNK, CHUNK, B_GROUPS, B_GROUP_W)
    mixed = jnp.einsum('gpq,bnqgc->bnpgc', w_s, v) + b_s.T[None, None, :, :, None]
    return (u * mixed.reshape(bsz, n, B_WIDTH)) @ w_o


def chunk_gmlp_mixer(h_lat, h_ctx, w_in, b_in, ln_g, ln_b, w_s, b_s, w_o, ctx_out):
    y_lat = chunk_gmlp(h_lat, w_in, b_in, ln_g, ln_b, w_s, b_s, w_o)
    y_ctx = chunk_gmlp(h_ctx, w_in, b_in, ln_g, ln_b, w_s, b_s, w_o) if ctx_out else None
    return y_lat, y_ctx


def global_qknorm_mixer(h_lat, h_ctx, w_qkv, w_o, q_g, k_g, cos, sin, ctx_out):
    bsz, n, _ = h_lat.shape
    n_ctx = h_ctx.shape[1]
    nb = n // BLOCK
    scale = C_HEAD_DIM ** -0.5
    q, k, v = jnp.split(h_lat @ w_qkv, [C_Q_DIM, C_Q_DIM + C_KV_DIM], axis=-1)
    q = apply_rope(rmsnorm(q.reshape(bsz, n, C_HEADS, C_HEAD_DIM), q_g), cos, sin) * scale
    k = apply_rope(rmsnorm(k.reshape(bsz, n, C_KV_HEADS, C_HEAD_DIM), k_g), cos, sin)
    v = v.reshape(bsz, n, C_KV_HEADS, C_HEAD_DIM)
    k_c, v_c = jnp.split(h_ctx @ w_qkv[:, C_Q_DIM:], 2, axis=-1)
    k_c = rmsnorm(k_c.reshape(bsz, n_ctx, C_KV_HEADS, C_HEAD_DIM), k_g)
    v_c = v_c.reshape(bsz, n_ctx, C_KV_HEADS, C_HEAD_DIM)

    qb = jnp.moveaxis(q.reshape(bsz, nb, BLOCK, C_KV_HEADS, C_GROUP, C_HEAD_DIM), 1, 0)

    def attend_block(q_blk):
        s_lat = jnp.einsum('bqhgd,bkhd->bhgqk', q_blk, k)
        s_ctx = jnp.einsum('bqhgd,bkhd->bhgqk', q_blk, k_c)
        return joint_softmax_attend([s_lat, s_ctx], [v, v_c])

    o = lax.map(attend_block, qb)
    y_lat = jnp.moveaxis(o, 0, 1).reshape(bsz, n, C_Q_DIM) @ w_o
    y_ctx = None
    if ctx_out:
        q_c = rmsnorm((h_ctx @ w_qkv[:, :C_Q_DIM]).reshape(bsz, n_ctx, C_HEADS, C_HEAD_DIM), q_g) * scale
        q_c = q_c.reshape(bsz, n_ctx, C_KV_HEADS, C_GROUP, C_HEAD_DIM)
        s_cc = jnp.einsum('bqhgd,bkhd->bhgqk', q_c, k_c)
        y_ctx = joint_softmax_attend([s_cc], [v_c]).reshape(bsz, n_ctx, C_Q_DIM) @ w_o
    return y_lat, y_ctx


def swiglu(h, w_in, w_out):
    gate, up = jnp.split(h @ w_in, 2, axis=-1)
    return (jax.nn.silu(gate) * up) @ w_out


def setup_inputs(seed: int = 0) -> dict:
    key = jax.random.key(seed)
    ks = jax.random.split(key, 24)

    def nrm(k, shape, scale):
        return jax.random.normal(k, shape, jnp.float32) * scale

    d = D_MODEL
    return {
        'x': nrm(ks[0], (BATCH, SEQ, d), 1.0),
        'c': nrm(ks[1], (BATCH, d), 1.0),
        'ctx': nrm(ks[2], (BATCH, CTX_LEN, d), 1.0),
        'c_ctx': nrm(ks[3], (d,), 1.0),
        'ada_w': nrm(ks[4], (DEPTH, d, 6 * d), 0.5 * d ** -0.5),
        'ada_b': nrm(ks[5], (DEPTH, 6 * d), 0.01),
        'norm_g': 1.0 + nrm(ks[6], (DEPTH, 4, d), 0.01),
        'ffn_w_in': nrm(ks[7], (DEPTH, d, 2 * FFN_HIDDEN), d ** -0.5),
        'ffn_w_out': nrm(ks[8], (DEPTH, FFN_HIDDEN, d), FFN_HIDDEN ** -0.5),
        'a_w_qkv': nrm(ks[9], (N_A_LAYERS, d, A_QKV_DIM), d ** -0.5),
        'a_w_o': nrm(ks[10], (N_A_LAYERS, A_Q_DIM, d), A_Q_DIM ** -0.5),
        'a_sink': nrm(ks[11], (N_A_LAYERS, A_HEADS), 0.5),
        'b_w_in': nrm(ks[12], (N_B_LAYERS, d, 2 * B_WIDTH), d ** -0.5),
        'b_b_in': nrm(ks[13], (N_B_LAYERS, 2 * B_WIDTH), 0.01),
        'b_ln_g': 1.0 + nrm(ks[14], (N_B_LAYERS, B_WIDTH), 0.01),
        'b_ln_b': nrm(ks[15], (N_B_LAYERS, B_WIDTH), 0.01),
        'b_w_s': nrm(ks[16], (N_B_LAYERS, B_GROUPS, CHUNK, CHUNK), CHUNK ** -0.5),
        'b_b_s': 1.0 + nrm(ks[17], (N_B_LAYERS, B_GROUPS, CHUNK), 0.02),
        'b_w_o': nrm(ks[18], (N_B_LAYERS, B_WIDTH, d), B_WIDTH ** -0.5),
        'c_w_qkv': nrm(ks[19], (N_C_LAYERS, d, C_QKV_DIM), d ** -0.5),
        'c_w_o': nrm(ks[20], (N_C_LAYERS, C_Q_DIM, d), C_Q_DIM ** -0.5),
        'c_q_g': 1.0 + nrm(ks[21], (N_C_LAYERS, C_HEAD_DIM), 0.01),
        'c_k_g': 1.0 + nrm(ks[22], (N_C_LAYERS, C_HEAD_DIM), 0.01),
    }


def reference(x, c, ctx, c_ctx, ada_w, ada_b, norm_g, ffn_w_in, ffn_w_out,
              a_w_qkv, a_w_o, a_sink,
              b_w_in, b_b_in, b_ln_g, b_ln_b, b_w_s, b_b_s, b_w_o,
              c_w_qkv, c_w_o, c_q_g, c_k_g):
    n = x.shape[1]
    cos_a, sin_a = axial_rope_tables(n, A_HEAD_DIM)
    cos_c, sin_c = axial_rope_tables(n, C_HEAD_DIM)
    silu_c = jax.nn.silu(c)
    silu_cc = jax.nn.silu(c_ctx)
    ctx_s = ctx
    for i in range(DEPTH):
        ctx_out = i < DEPTH - 1
        kind = i % N_MIXERS
        j = i // N_MIXERS
        mod_l = silu_c @ ada_w[i] + ada_b[i]
        mod_c = silu_cc @ ada_w[i] + ada_b[i]
        sh_ml, sc_ml, g_ml, sh_fl, sc_fl, g_fl = [m[:, None, :] for m in jnp.split(mod_l, 6, axis=-1)]
        sh_mc, sc_mc, g_mc, sh_fc, sc_fc, g_fc = jnp.split(mod_c, 6, axis=-1)

        h_l = modulate(rmsnorm(x, norm_g[i, 0]), sh_ml, sc_ml)
        h_c = modulate(rmsnorm(ctx_s, norm_g[i, 0]), sh_mc, sc_mc) if (ctx_out or kind != 1) else None
        if kind == 0:
            y_l, y_c = window_sink_mixer(h_l, h_c, a_w_qkv[j], a_w_o[j], a_sink[j], cos_a, sin_a, ctx_out)
        elif kind == 1:
            y_l, y_c = chunk_gmlp_mixer(h_l, h_c, b_w_in[j], b_b_in[j], b_ln_g[j], b_ln_b[j],
                                        b_w_s[j], b_b_s[j], b_w_o[j], ctx_out)
        else:
            y_l, y_c = global_qknorm_mixer(h_l, h_c, c_w_qkv[j], c_w_o[j], c_q_g[j], c_k_g[j],
                                           cos_c, sin_c, ctx_out)

        x = x + g_ml * rmsnorm(y_l, norm_g[i, 1])
        f_l = swiglu(modulate(rmsnorm(x, norm_g[i, 2]), sh_fl, sc_fl), ffn_w_in[i], ffn_w_out[i])
        x = x + g_fl * rmsnorm(f_l, norm_g[i, 3])

        if ctx_out:
            ctx_s = ctx_s + g_mc * rmsnorm(y_c, norm_g[i, 1])
            f_c = swiglu(modulate(rmsnorm(ctx_s, norm_g[i, 2]), sh_fc, sc_fc), ffn_w_in[i], ffn_w_out[i])
            ctx_s = ctx_s + g_fc * rmsnorm(f_c, norm_g[i, 3])
    return x
```

```python
from contextlib import ExitStack
import math
import numpy as np
import concourse.bass as bass
import concourse.mybir as mybir
from concourse.bass_utils import run_bass_kernel_spmd

F32 = mybir.dt.float32
BF16 = mybir.dt.bfloat16
AF = mybir.ActivationFunctionType
ALU = mybir.AluOpType

D = 1024
DC = 8
NLAT = 2048
NCTX = 256
T = NLAT + NCTX
DEPTH = 4
FH = 2816
HC = 22
RMS_EPS = 1e-6
LN_EPS = 1e-5
TILES = [(0, 512), (512, 512), (1024, 512), (1536, 512), (2048, 256)]
NCORES = 8


class Tok:
    __slots__ = ("name", "last_w", "readers", "alias")

    def __init__(self, name):
        self.name = name
        self.last_w = None
        self.readers = []
        self.alias = ()


class Op:
    __slots__ = ("eng", "emit", "dma", "key", "n", "deps", "signal", "sem", "val", "waits", "pos")


SEM_ROT = 30000


class Prog:
    def __init__(self):
        self.ops = []
        self.per_eng = {}

    def _add(self, eng, emit, reads, writes, dma=False, key=None, n=1):
        op = Op()
        op.eng = eng
        op.emit = emit
        op.dma = dma
        op.key = key
        op.n = n
        op.signal = False
        op.sem = None
        op.val = 0
        lst = self.per_eng.setdefault(eng, [])
        op.pos = len(lst)
        lst.append(op)
        deps = set()
        rs = []
        for t in reads:
            rs.append(t)
            rs.extend(t.alias)
        ws = []
        for t in writes:
            ws.append(t)
            ws.extend(t.alias)
        for t in rs:
            if t.last_w is not None:
                deps.add(t.last_w)
        for t in ws:
            if t.last_w is not None:
                deps.add(t.last_w)
            deps.update(t.readers)
        for t in rs:
            t.readers.append(op)
        for t in ws:
            t.last_w = op
            t.readers = []
        deps.discard(op)
        fin = []
        for d in deps:
            if (not d.dma) and (not dma) and d.eng == eng:
                if eng == "pe":
                    continue
            d.signal = True
            fin.append(d)
        op.deps = fin
        self.ops.append(op)
        return op

    def op(self, eng, emit, reads=(), writes=()):
        return self._add(eng, emit, reads, writes)

    def dma(self, queue, emit, reads=(), writes=(), key="dma", n=1):
        return self._add(queue, emit, reads, writes, dma=True, key=key, n=n)

    def barrier(self, engs=("pe", "act", "dve", "pool")):
        bt = [Tok("bar") for _ in engs]
        for e, t in zip(engs, bt):
            self.op(e, lambda x: x.drain(), writes=[t])
        for e in engs:
            self.op(e, lambda x: x.nop(), reads=bt)

    def build(self, nc):
        stack = ExitStack()
        nsem = [0]

        def newsem(name):
            nsem[0] += 1
            return stack.enter_context(nc.semaphore("%s_%d" % (name, nsem[0])))

        cur = {}
        cnt = {}
        for op in self.ops:
            if not op.signal:
                continue
            k = ("d", op.key) if op.dma else ("e", op.eng)
            inc = 16 * op.n if op.dma else 1
            if k not in cur or cnt[k] + inc > SEM_ROT:
                cur[k] = newsem("s" + str(k[1]))
                cnt[k] = 0
            cnt[k] += inc
            op.sem = cur[k]
            op.val = cnt[k]
        seen = {}
        for op in self.ops:
            sd = seen.setdefault(op.eng, {})
            need = {}
            for d in op.deps:
                sid = id(d.sem)
                if sd.get(sid, 0) >= d.val:
                    continue
                if sid not in need or need[sid][1] < d.val:
                    need[sid] = (d.sem, d.val)
            op.waits = list(need.values())
            for sid, (s, v) in need.items():
                sd[sid] = v
        self.nsem = nsem[0]
        engs = self.per_eng
        with stack:
            with nc.Block() as block:
                def run(e, name):
                    for op in engs.get(name, []):
                        for (s, v) in op.waits:
                            e.wait_ge(s, v)
                        ins = op.emit(e)
                        if not isinstance(ins, (list, tuple)):
                            ins = [ins]
                        if op.signal:
                            if op.dma:
                                assert len(ins) == op.n, (len(ins), op.n)
                                for i in ins:
                                    i.then_inc(op.sem, 16)
                            else:
                                ins[-1].then_inc(op.sem, 1)

                @block.tensor
                def _(e):
                    run(e, "pe")

                @block.scalar
                def _(e):
                    run(e, "act")

                @block.vector
                def _(e):
                    run(e, "dve")

                @block.gpsimd
                def _(e):
                    run(e, "pool")

                @block.sync
                def _(e):
                    run(e, "sp")


class Rot:
    def __init__(self, items):
        self.items = items
        self.i = 0

    def next(self):
        r = self.items[self.i % len(self.items)]
        self.i += 1
        return r


def build_program(NB=4, NL=DEPTH, do_mixer=True, do_ffn=True):
    nc = bass.Bass("TRN2", target_bir_lowering=False)
    P = Prog()
    es = ExitStack()

    def dram(name, shape, kind="ExternalInput", dt=F32):
        return nc.dram_tensor(name, list(shape), dt, kind=kind).ap()

    def sb(name, shape, dt):
        return es.enter_context(nc.sbuf_tensor(name, list(shape), dt))

    d_xT = dram("xT", [NB, 128, DC, T])
    d_cT = dram("cT", [128, DC, NB + 1])
    d_adaw = dram("adaw", [DEPTH, 48, 128, DC, 128])
    d_adab = dram("adab", [128, DEPTH, 48])
    d_ng = dram("ng", [128, DEPTH, 4, DC])
    d_fin = dram("fin", [DEPTH, 2 * HC, 128, DC, 128])
    d_fout = dram("fout", [DEPTH, DC, 3, 128, 8, 128])
    d_aw = dram("aw", [2, 20, 128, DC, 128])
    d_asink = dram("asink", [128, 2, DC])
    d_bw = dram("bw", [24, 128, DC, 128])
    d_bbu = dram("bbu", [128, DC])
    d_bbc = dram("bbc", [3, 128, 1024])
    d_bws = dram("bws", [128, 8, 128])
    d_bbs = dram("bbs", [128, 8, 128])
    d_cw = dram("cw", [24, 128, DC, 128])
    d_cg = dram("cg", [128, 2])
    d_rope = dram("rope", [2, 2, 128, NLAT])
    d_cst = dram("cst", [5, 128, 128])
    d_out = dram("outT", [NB, 128, DC, NLAT], kind="ExternalOutput")

    NWB = 4
    X = sb("X", [128, DC, T], F32)
    U = sb("U", [128, 24320], F32)
    WS = [sb("WS%d" % i, [128, 1024], F32) for i in range(2)]
    WB = [sb("WB%d" % i, [128, 8, 128], BF16) for i in range(NWB)]
    TMP = [sb("TMP%d" % i, [128, 512], F32) for i in range(3)]
    RS = [sb("RS%d" % i, [128, 512], F32) for i in range(2)]
    SQ = [sb("SQ%d" % i, [128, 512], BF16) for i in range(2)]
    PT = [sb("PT%d" % i, [128, 512], BF16) for i in range(3)]
    MOD = sb("MOD", [128, DEPTH, 48, NB + 1], F32)
    DER = [sb("DER%d" % i, [128, 2, 4, DC], F32) for i in range(2)]
    NG = sb("NG", [128, DEPTH, 4, DC], F32)
    ADAB = sb("ADAB", [128, DEPTH, 48], F32)
    CST32 = sb("CST32", [128, 128], F32)
    CST = sb("CST", [128, 6, 128], BF16)
    SC = sb("SC", [128, DC, NB + 1], BF16)
    C32 = sb("C32", [128, DC, NB + 1], F32)
    EPS = sb("EPS", [128, 4], F32)
    ESINK = sb("ESINK", [128, 2, DC], F32)
    BBU = sb("BBU", [128, DC], F32)
    CG = sb("CG", [128, 2], F32)
    LNS = sb("LNS", [128, 16], F32)
    PS = [es.enter_context(nc.psum_tensor("PS%d" % i, [128, 512], F32)) for i in range(8)]

    def toks(name, *dims):
        if len(dims) == 1:
            return [Tok("%s%d" % (name, i)) for i in range(dims[0])]
        return [toks("%s%d_" % (name, i), *dims[1:]) for i in range(dims[0])]

    tX = toks("X", DC, 5)
    tWS = toks("WS", 2)
    tWB = toks("WB", NWB)
    tPS = toks("PS", 8)
    rWS = Rot(list(zip(WS, tWS, range(2))))
    rWB = Rot(list(zip(WB, tWB)))
    rTMP = Rot(list(zip(TMP, toks("TMP", 3))))
    rRS = Rot(list(zip(RS, toks("RS", 2))))
    rSQ = Rot(list(zip(SQ, toks("SQ", 2))))
    rPT = Rot(list(zip(PT, toks("PT", 3))))
    rMAIN = Rot([(PS[i], tPS[i]) for i in range(4)])
    rAUX = Rot([(PS[i], tPS[i]) for i in range(4, 7)])
    tROPE = Tok("ROPE")
    tMOD = toks("MOD", DEPTH)
    tLNS = Tok("LNS")
    tDER = toks("DER", 2)
    tC = Tok("consts")
    tOUT = Tok("out")
    ones = CST[:, 0, :]
    permA = CST[:, 1, :]
    permC = CST[:, 2, :]
    ident = CST[:, 3, :]
    mask_next = CST[:, 4, :]
    mask_prev = CST[:, 5, :]

    def ubf(a, b, **kw):
        v = U[:, a:b].bitcast(BF16)
        return v

    H = ubf(0, 9216).rearrange("p (c t) -> p c t", c=DC)
    tH = toks("H", DC, 5)
    QO = ubf(9216, 18432).rearrange("p (c t) -> p c t", c=DC)
    tQO = toks("QO", DC, 5)
    KT = ubf(18432, 20736).rearrange("p (c t) -> p c t", c=2)
    tKT = toks("KT", 2, 5)
    VV = ubf(20736, 23040).rearrange("p (b n) -> p b n", b=18)
    tVV = toks("VV", 18)
    ROPE = U[:, 23040:24064].rearrange("p (k n) -> p k n", k=2)
    GN = 1280
    Fst = U[:, 0:10240].rearrange("p (c t) -> p c t", c=DC)
    tF = toks("F", DC, 3)
    Hg = ubf(0, 5120).rearrange("p (c t) -> p c t", c=DC)
    tHg = toks("Hg", DC, 3)
    Hid = ubf(10240, 24320).rearrange("p (c t) -> p c t", c=HC)
    tHid = toks("Hid", HC, 3)
    allHg = [t for r in tHg for t in r]
    allF = [t for r in tF for t in r]
    for t in allHg:
        t.alias = tuple(allF)
    for t in allF:
        t.alias = tuple(allHg)
    uT = ubf(9216, 14336).rearrange("p (c t) -> p c t", c=DC)
    tuT = toks("uT", DC, 3)
    vtm = ubf(14336, 19456).rearrange("p (b n) -> p b n", b=10)
    tvtm = toks("vtm", 10)
    BBC = U[:, 19456:22528].rearrange("p (k n) -> p k n", k=3)
    BBS = U[:, 22528:23552].rearrange("p (g n) -> p g n", g=8)
    WST = ubf(23552, 24064).rearrange("p (g n) -> p g n", g=8)
    tBC = Tok("bconst")

    def ACT(out, in_, func, reads, writes, **kw):
        P.op("act", lambda e: e.activation(out=out, in_=in_, func=func, **kw), reads=reads, writes=writes)

    def TT(eng, out, in0, in1, op, reads, writes):
        P.op(eng, lambda e: e.tensor_tensor(out=out, in0=in0, in1=in1, op=op), reads=reads, writes=writes)

    def TS(eng, out, in0, s1, s2, op0, op1, reads, writes):
        P.op(eng, lambda e: e.tensor_scalar(out=out, in0=in0, scalar1=s1, scalar2=s2, op0=op0, op1=op1), reads=reads, writes=writes)

    def STT(eng, out, in0, scalar, in1, op0, op1, reads, writes):
        P.op(eng, lambda e: e.scalar_tensor_tensor(out=out, in0=in0, scalar=scalar, in1=in1, op0=op0, op1=op1), reads=reads, writes=writes)

    def CP(eng, out, in_, reads, writes):
        P.op(eng, lambda e: e.tensor_copy(out=out, in_=in_), reads=reads, writes=writes)

    def RECIP(out, in_, reads, writes):
        P.op("dve", lambda e: e.reciprocal(out=out, in_=in_), reads=reads, writes=writes)

    def MM1(out, lhsT, rhs, start, stop, reads, writes):
        P.op("pe", lambda e: e.matmul(out, lhsT, rhs, start=start, stop=stop), reads=reads, writes=writes)

    def DMA(queue, out, in_, reads, writes, key):
        P.dma(queue, lambda e: [e.dma_start(out=out, in_=in_)], reads=reads, writes=writes, key=key)

    def wload(src, C=8, eng="pool"):
        ws, tws, wi = rWS.next()
        wb, twb = rWB.next()
        n = C * 128
        DMA("sp", ws[:, 0:n], src.rearrange("p c n -> p (c n)"), [], [tws], "ws%d" % wi)
        if eng == "act":
            ACT(wb[:, 0:C, :].rearrange("p c n -> p (c n)"), ws[:, 0:n], AF.Copy, [tws], [twb])
        else:
            CP(eng, wb[:, 0:C, :].rearrange("p c n -> p (c n)"), ws[:, 0:n], [tws], [twb])
        return wb, twb

    def mm(ps, tps, pairs, reads):
        pairs = list(pairs)

        def emit(e):
            r = []
            k = len(pairs)
            for i, (l, rr) in enumerate(pairs):
                r.append(e.matmul(ps, l, rr, start=(i == 0), stop=(i == k - 1)))
            return r
        P.op("pe", emit, reads=reads, writes=[tps])

    def rstd_from_ps(ps, tps, n, scale, eps_col):
        rs, trs = rRS.next()
        ACT(rs[:, 0:n], ps[:, 0:n], AF.Sqrt, [tps, tC], [trs], scale=scale, bias=EPS[:, eps_col:eps_col + 1])
        RECIP(rs[:, 0:n], rs[:, 0:n], [trs], [trs])
        return rs, trs

    def prenorm(ti_list, dst, tdst, dst_off, der, tder, kind_a, l, shift_k, col_lat, col_ctx):
        for k, ti in enumerate(ti_list):
            s0, n = TILES[ti]
            lc = 1 if ti == 4 else 0
            col = col_ctx if ti == 4 else col_lat
            ps, tps = rAUX.next()
            for c in range(DC):
                sq, tsq = rSQ.next()
                if c % 2 == 0:
                    ACT(sq[:, 0:n], X[:, c, s0:s0 + n], AF.Square, [tX[c][ti]], [tsq])
                else:
                    TT("dve", sq[:, 0:n], X[:, c, s0:s0 + n], X[:, c, s0:s0 + n], ALU.mult, [tX[c][ti]], [tsq])
                MM1(ps[:, 0:n], ones, sq[:, 0:n], c == 0, c == DC - 1, [tsq, tC], [tps])
            rs, trs = rstd_from_ps(ps, tps, n, 1.0 / D, 0)
            o = dst_off[k]
            for c in range(DC):
                tmp, ttmp = rTMP.next()
                STT("dve", tmp[:, 0:n], X[:, c, s0:s0 + n], der[:, lc, kind_a, c:c + 1], rs[:, 0:n], ALU.mult, ALU.mult,
                    [tX[c][ti], trs, tder], [ttmp])
                ACT(dst[:, c, o:o + n], tmp[:, 0:n], AF.Identity, [ttmp, tMOD[l]], [tdst[c][k]],
                    bias=MOD[:, l, shift_k * 8 + c, col:col + 1])

    def postnorm_stage(ps, tps, j, n, fdst, tfd, ss, tss, gam, tgam):
        ACT(fdst, ps[:, 0:n], AF.Copy, [tps, tgam], [tfd], scale=gam)
        sq, tsq = rSQ.next()
        ACT(sq[:, 0:n], ps[:, 0:n], AF.Square, [tps], [tsq])
        MM1(ss[:, 0:n], ones, sq[:, 0:n], j == 0, j == DC - 1, [tsq, tC], [tss])

    def postnorm_apply(ti, n, fsrc, tfs, ss, tss, der, tder, kind_g):
        s0, _ = TILES[ti]
        lc = 1 if ti == 4 else 0
        rs, trs = rstd_from_ps(ss, tss, n, 1.0 / D, 0)
        for j in range(DC):
            TT("dve", fsrc[j], fsrc[j], rs[:, 0:n], ALU.mult, [tfs[j], trs], [tfs[j]])
        for j in range(DC):
            TT("dve", X[:, j, s0:s0 + n], X[:, j, s0:s0 + n], fsrc[j], ALU.add, [tfs[j], tX[j][ti]], [tX[j][ti]])

    def setup():
        P.dma("act", lambda e: [e.dma_start(out=NG[:], in_=d_ng), e.dma_start(out=ADAB[:], in_=d_adab),
                                e.dma_start(out=C32[:], in_=d_cT), e.dma_start(out=ESINK[:], in_=d_asink),
                                e.dma_start(out=BBU[:], in_=d_bbu), e.dma_start(out=CG[:], in_=d_cg)],
              writes=[tC], key="c0", n=6)
        P.op("dve", lambda e: [e.memset(EPS[:, 0:1], RMS_EPS), e.memset(EPS[:, 1:2], LN_EPS), e.memset(CST[:, 0, :], 1.0)],
             writes=[tC])
        tc32 = Tok("cst32")
        for k in range(5):
            DMA("act", CST32[:], d_cst[k], [], [tc32], "c1")
            CP("dve", CST[:, k + 1, :], CST32[:], [tc32], [tC])
        ACT(SC[:], C32[:], AF.Silu, [tC], [tC])
        ACT(ESINK[:], ESINK[:], AF.Exp, [tC], [tC])
        for l in range(NL):
            for j in range(48):
                wb, twb = wload(d_adaw[l, j], eng=("pool", "dve", "act", "dve")[j % 4])
                ps, tps = rMAIN.next()
                mm(ps[:, 0:NB + 1], tps, [(wb[:, c, :], SC[:, c, :]) for c in range(DC)], reads=[twb, tC])
                ACT(MOD[:, l, j, :], ps[:, 0:NB + 1], AF.Identity, [tps, tC], [tMOD[l]], bias=ADAB[:, l, j:j + 1])

    def derive(l, b, slot):
        der, tder = DER[slot], tDER[slot]
        for lc, col in ((0, b), (1, NB)):
            for (k, sck, ngk, isgate) in ((0, 1, 0, False), (1, 2, 1, True), (2, 4, 2, False), (3, 5, 3, True)):
                src = MOD[:, l, sck * 8:(sck + 1) * 8, col]
                if isgate:
                    TT("dve", der[:, lc, k, :], src, NG[:, l, ngk, :], ALU.mult, [tMOD[l], tC], [tder])
                else:
                    STT("dve", der[:, lc, k, :], src, 1.0, NG[:, l, ngk, :], ALU.add, ALU.mult, [tMOD[l], tC], [tder])
        return der, tder

    def ffn(l, b, der, tder, with_ctx):
        groups = [[0, 1], [2, 3, 4] if with_ctx else [2, 3]]
        for G in groups:
            offs = []
            o = 0
            for ti in G:
                offs.append(o)
                o += TILES[ti][1]
            tHgG = [[tHg[c][k] for k in range(len(G))] for c in range(DC)]
            prenorm(G, Hg, tHgG, offs, der, tder, 2, l, 3, b, NB)
            for hc in range(HC):
                wg, twg = wload(d_fin[l, 2 * hc])
                wu, twu = wload(d_fin[l, 2 * hc + 1], eng="dve")
                for k, ti in enumerate(G):
                    n = TILES[ti][1]
                    o = offs[k]
                    rd = [tHg[c][k] for c in range(DC)]
                    pg, tpg = rMAIN.next()
                    mm(pg[:, 0:n], tpg, [(wg[:, c, :], Hg[:, c, o:o + n]) for c in range(DC)], reads=[twg] + rd)
                    pu, tpu = rMAIN.next()
                    mm(pu[:, 0:n], tpu, [(wu[:, c, :], Hg[:, c, o:o + n]) for c in range(DC)], reads=[twu] + rd)
                    tmp, ttmp = rTMP.next()
                    ACT(tmp[:, 0:n], pg[:, 0:n], AF.Silu, [tpg], [ttmp])
                    TT("dve", Hid[:, hc, o:o + n], tmp[:, 0:n], pu[:, 0:n], ALU.mult, [ttmp, tpu], [tHid[hc][k]])
            ssb = [(PS[4 + k], tPS[4 + k]) for k in range(len(G))]
            for j in range(DC):
                parts = [wload(d_fout[l, j, pt, :, 0:(8 if pt < 2 else 6), :], C=(8 if pt < 2 else 6), eng=("dve" if pt == 1 else "pool")) for pt in range(3)]
                for k, ti in enumerate(G):
                    n = TILES[ti][1]
                    o = offs[k]
                    py, tpy = rMAIN.next()
                    mm(py[:, 0:n], tpy, [(parts[hc // 8][0][:, hc % 8, :], Hid[:, hc, o:o + n]) for hc in range(HC)],
                       reads=[p[1] for p in parts] + [tHid[hc][k] for hc in range(HC)])
                    postnorm_stage(py, tpy, j, n, Fst[:, j, o:o + n], tF[j][k], ssb[k][0], ssb[k][1],
                                   der[:, 1 if ti == 4 else 0, 3, j:j + 1], tder)
            for k, ti in enumerate(G):
                n = TILES[ti][1]
                o = offs[k]
                postnorm_apply(ti, n, [Fst[:, j, o:o + n] for j in range(DC)], [tF[j][k] for j in range(DC)],
                               ssb[k][0], ssb[k][1], der, tder, 3)

    tFm = toks("Fm", DC, 2)
    rACC = Rot([((PS[4], tPS[4]), (PS[5], tPS[5])), ((PS[6], tPS[6]), (PS[7], tPS[7]))])
    EXP_A = 0.125
    EXP_C = 128.0 ** -0.5

    def out_proj(wsrc, O, tO, groups, der, tder, Fbase, tFx, nslot=1):
        for gi, G in enumerate(groups):
            if nslot == 2:
                assert len(G) == 1
                sl = gi % 2
                Fm = U[:, Fbase + 4096 * sl:Fbase + 4096 * (sl + 1)].rearrange("p (c t) -> p c t", c=DC)
                tFg = [[tFx[j][sl]] for j in range(DC)]
                ssb = [(PS[4 + sl], tPS[4 + sl])]
            else:
                Fm = U[:, Fbase:Fbase + 8192].rearrange("p (c t) -> p c t", c=DC)
                tFg = tFx
                ssb = [(PS[4 + k], tPS[4 + k]) for k in range(len(G))]
            offs = []
            o = 0
            for ti in G:
                offs.append(o)
                o += TILES[ti][1]
            for j in range(DC):
                wo, two = wload(wsrc(j))
                for k, ti in enumerate(G):
                    s0, n = TILES[ti]
                    py, tpy = rMAIN.next()
                    mm(py[:, 0:n], tpy, [(wo[:, c, :], O(c, ti)) for c in range(DC)], reads=[two] + [tO(c, ti) for c in range(DC)])
                    postnorm_stage(py, tpy, j, n, Fm[:, j, offs[k]:offs[k] + n], tFg[j][k], ssb[k][0], ssb[k][1],
                                   der[:, 1 if ti == 4 else 0, 1, j:j + 1], tder)
            for k, ti in enumerate(G):
                n = TILES[ti][1]
                postnorm_apply(ti, n, [Fm[:, j, offs[k]:offs[k] + n] for j in range(DC)], [tFg[j][k] for j in range(DC)],
                               ssb[k][0], ssb[k][1], der, tder, 1)

    def rope_pass(which, chunks):
        perm = permA if which == 0 else permC
        for ti in range(4):
            s0, n = TILES[ti]
            P.dma("act", lambda e, s0=s0, n=n: [e.dma_start(out=ROPE[:, 0, :], in_=d_rope[which, 0, :, s0:s0 + n]),
                                                 e.dma_start(out=ROPE[:, 1, :], in_=d_rope[which, 1, :, s0:s0 + n])],
                  writes=[tROPE], key="rope", n=2)
            for (buf, c, trow) in chunks:
                src = buf[:, c, s0:s0 + n]
                tk = trow[ti]
                ps, tps = rAUX.next()
                MM1(ps[:, 0:n], perm, src, True, True, [tk, tC], [tps])
                t1, tt1 = rTMP.next()
                TT("pool", t1[:, 0:n], src, ROPE[:, 0, :], ALU.mult, [tk, tROPE], [tt1])
                t2, tt2 = rTMP.next()
                TT("dve", t2[:, 0:n], ps[:, 0:n], ROPE[:, 1, :], ALU.mult, [tps, tROPE], [tt2])
                TT("pool", src, t1[:, 0:n], t2[:, 0:n], ALU.add, [tt1, tt2], [tk])

    def v_proj(wlist, tiles, nblk_cols):
        for ti in tiles:
            s0, n = TILES[ti]
            for bi in range(n // 128):
                blk = s0 // 128 + bi
                ps, tps = rMAIN.next()
                for s, (wv, twv) in enumerate(wlist):
                    mm(ps[:, s * 128:(s + 1) * 128], tps, [(H[:, c, s0 + bi * 128:s0 + (bi + 1) * 128], wv[:, c, :]) for c in range(DC)],
                       reads=[twv] + [tH[c][ti] for c in range(DC)])
                CP("dve", VV[:, blk, 0:nblk_cols], ps[:, 0:nblk_cols], [tps], [tVV[blk]])

    def mixer_a(l, b, der, tder, ctx_out):
        la = l // 3
        allt = [0, 1, 2, 3, 4]
        qt = allt if ctx_out else [0, 1, 2, 3]
        prenorm(allt, H, [[tH[c][ti] for ti in allt] for c in range(DC)], [TILES[ti][0] for ti in allt], der, tder, 0, l, 0, b, NB)
        for j in range(DC):
            wq, twq = wload(d_aw[la, j])
            for ti in qt:
                s0, n = TILES[ti]
                ps, tps = rMAIN.next()
                mm(ps[:, 0:n], tps, [(wq[:, c, :], H[:, c, s0:s0 + n]) for c in range(DC)], reads=[twq] + [tH[c][ti] for c in range(DC)])
                ACT(QO[:, j, s0:s0 + n], ps[:, 0:n], AF.Copy, [tps], [tQO[j][ti]])
        for g in range(2):
            wk, twk = wload(d_aw[la, 8 + g])
            for ti in allt:
                s0, n = TILES[ti]
                ps, tps = rMAIN.next()
                mm(ps[:, 0:n], tps, [(wk[:, c, :], H[:, c, s0:s0 + n]) for c in range(DC)], reads=[twk] + [tH[c][ti] for c in range(DC)])
                ACT(KT[:, g, s0:s0 + n], ps[:, 0:n], AF.Copy, [tps], [tKT[g][ti]])
        rope_pass(0, [(QO, j, tQO[j]) for j in range(DC)] + [(KT, g, tKT[g]) for g in range(2)])
        wl = [wload(d_aw[la, 10]), wload(d_aw[la, 11])]
        v_proj(wl, allt, 256)
        for j in range(DC):
            kc = j // 4
            for ti in qt:
                (acc, tacc), (den, tden) = rACC.next()
                s0, n = TILES[ti]
                qb0 = s0 // 128
                if ti < 4:
                    kbs = list(range(max(0, qb0 - 1), min(15, qb0 + 4) + 1)) + [16, 17]
                else:
                    kbs = [16, 17]
                for hf in range(2):
                    g = 2 * kc + hf
                    r0, r1 = hf * 64, (hf + 1) * 64
                    pend = None

                    def pv(item, first):
                        pt, tpt, kb, c0, c1 = item
                        kti = min(kb // 4, 4)
                        def emit(e, pt=pt, kb=kb, c0=c0, c1=c1, first=first, r0=r0, r1=r1, g=g, acc=acc, den=den):
                            return [e.matmul(acc[r0:r1, c0:c1], VV[:, kb, g * 64:(g + 1) * 64], pt[:, c0:c1], start=first, stop=False, skip_group_check=True),
                                    e.matmul(den[r0:r1, c0:c1], ones[:, 0:64], pt[:, c0:c1], start=first, stop=False, skip_group_check=True)]
                        P.op("pe", emit, reads=[tpt, tVV[kb], tC], writes=[tacc, tden])

                    first = True
                    for kb in kbs:
                        if kb < 16 and ti < 4:
                            cb0, cb1 = max(kb - 1, qb0), min(kb + 1, qb0 + 3)
                            c0, c1 = (cb0 - qb0) * 128, (cb1 - qb0 + 1) * 128
                        else:
                            cb0 = cb1 = None
                            c0, c1 = 0, n
                        kti = min(kb // 4, 4)
                        ps, tps = rMAIN.next()
                        pairs = [(ps[:, c0:c1], KT[r0:r1, kc, kb * 128:(kb + 1) * 128], QO[r0:r1, j, s0 + c0:s0 + c1])]
                        if cb0 is not None:
                            if cb0 <= kb - 1 <= cb1:
                                cc = (kb - 1 - qb0) * 128
                                pairs.append((ps[:, cc:cc + 128], ident, mask_next))
                            if cb0 <= kb + 1 <= cb1:
                                cc = (kb + 1 - qb0) * 128
                                pairs.append((ps[:, cc:cc + 128], ident, mask_prev))

                        def emit(e, pairs=pairs):
                            r = []
                            for i, (o_, l_, r_) in enumerate(pairs):
                                r.append(e.matmul(o_, l_, r_, start=(i == 0), stop=(i == len(pairs) - 1)))
                            return r
                        P.op("pe", emit, reads=[tKT[kc][kti], tQO[j][ti], tC], writes=[tps])
                        pt, tpt = rPT.next()
                        ACT(pt[:, c0:c1], ps[:, c0:c1], AF.Exp, [tps], [tpt], scale=EXP_A)
                        if pend is not None:
                            pv(pend, first)
                            first = False
                        pend = (pt, tpt, kb, c0, c1)
                    pv(pend, first)
                rs, trs = rRS.next()
                TS("dve", rs[:, 0:n], den[:, 0:n], ESINK[:, la, j:j + 1], None, ALU.add, ALU.bypass, [tden, tC], [trs])
                RECIP(rs[:, 0:n], rs[:, 0:n], [trs], [trs])
                TT("dve", QO[:, j, s0:s0 + n], acc[:, 0:n], rs[:, 0:n], ALU.mult, [tacc, trs], [tQO[j][ti]])
        groups = [[0], [1], [2], [3]] + ([[4]] if ctx_out else [])
        out_proj(lambda j: d_aw[la, 12 + j], lambda c, ti: QO[:, c, TILES[ti][0]:TILES[ti][0] + TILES[ti][1]], lambda c, ti: tQO[c][ti],
                 groups, der, tder, 0, tFm, nslot=2)

    def mixer_c(l, b, der, tder, ctx_out):
        allt = [0, 1, 2, 3, 4]
        qt = allt if ctx_out else [0, 1, 2, 3]
        prenorm(allt, H, [[tH[c][ti] for ti in allt] for c in range(DC)], [TILES[ti][0] for ti in allt], der, tder, 0, l, 0, b, NB)

        def proj_norm(wsrc, tiles, dst, dc, tdrow, gcol):
            w, tw = wload(wsrc)
            for ti in tiles:
                s0, n = TILES[ti]
                ps, tps = rMAIN.next()
                mm(ps[:, 0:n], tps, [(w[:, c, :], H[:, c, s0:s0 + n]) for c in range(DC)], reads=[tw] + [tH[c][ti] for c in range(DC)])
                sq, tsq = rSQ.next()
                ACT(sq[:, 0:n], ps[:, 0:n], AF.Square, [tps], [tsq])
                ss, tss = rAUX.next()
                MM1(ss[:, 0:n], ones, sq[:, 0:n], True, True, [tsq, tC], [tss])
                rs, trs = rstd_from_ps(ss, tss, n, 1.0 / 128, 0)
                STT("dve", dst[:, dc, s0:s0 + n], ps[:, 0:n], CG[:, gcol:gcol + 1], rs[:, 0:n], ALU.mult, ALU.mult, [tps, trs, tC], [tdrow[ti]])

        for h in range(DC):
            proj_norm(d_cw[h], qt, QO, h, tQO[h], 0)
        for kvh in range(2):
            for gi in range(2):
                proj_norm(d_cw[8 + 2 * kvh + gi], allt, KT, gi, tKT[gi], 1)
            chunks = [(KT, gi, tKT[gi]) for gi in range(2)]
            if kvh == 0:
                chunks = [(QO, h, tQO[h]) for h in range(DC)] + chunks
            rope_pass(1, chunks)
            wl = [wload(d_cw[12 + 2 * kvh]), wload(d_cw[12 + 2 * kvh + 1])]
            v_proj(wl, allt, 256)
            for gi in range(2):
                g = 2 * kvh + gi
                for h in (2 * g, 2 * g + 1):
                    for ti in qt:
                        (acc, tacc), (den, tden) = rACC.next()
                        s0, n = TILES[ti]
                        kbs = list(range(18)) if ti < 4 else [16, 17]
                        pend = None

                        def pv(item, first, last=False):
                            pt, tpt, kb = item
                            def emit(e, pt=pt, kb=kb, first=first, n=n, gi=gi, last=last, acc=acc, den=den):
                                return [e.matmul(acc[:, 0:n], VV[:, kb, gi * 128:(gi + 1) * 128], pt[:, 0:n], start=first, stop=last),
                                        e.matmul(den[:, 0:n], ones, pt[:, 0:n], start=first, stop=last)]
                            P.op("pe", emit, reads=[tpt, tVV[kb], tC], writes=[tacc, tden])

                        first = True
                        for kb in kbs:
                            kti = min(kb // 4, 4)
                            ps, tps = rMAIN.next()
                            MM1(ps[:, 0:n], KT[:, gi, kb * 128:(kb + 1) * 128], QO[:, h, s0:s0 + n], True, True,
                                [tKT[gi][kti], tQO[h][ti]], [tps])
                            pt, tpt = rPT.next()
                            ACT(pt[:, 0:n], ps[:, 0:n], AF.Exp, [tps], [tpt], scale=EXP_C)
                            if pend is not None:
                                pv(pend, first)
                                first = False
                            pend = (pt, tpt, kb)
                        pv(pend, first, True)
                        rs, trs = rRS.next()
                        RECIP(rs[:, 0:n], den[:, 0:n], [tden], [trs])
                        TT("dve", QO[:, h, s0:s0 + n], acc[:, 0:n], rs[:, 0:n], ALU.mult, [tacc, trs], [tQO[h][ti]])
        groups = [[0], [1], [2], [3]] + ([[4]] if ctx_out else [])
        out_proj(lambda j: d_cw[16 + j], lambda c, ti: QO[:, c, TILES[ti][0]:TILES[ti][0] + TILES[ti][1]], lambda c, ti: tQO[c][ti],
                 groups, der, tder, 0, tFm, nslot=2)

    BuT = ubf(0, 4096).rearrange("p (c t) -> p c t", c=DC)
    tBu = toks("Bu", DC, 2)
    BH = ubf(4096, 8192).rearrange("p (c t) -> p c t", c=DC)
    tBH = toks("BH", DC, 2)
    Bv = ubf(8192, 12288).rearrange("p (b n) -> p b n", b=8)
    tBv = toks("Bv", 8)
    tFb = toks("Fb", DC, 2)
    _al = [t for r in tBH for t in r] + list(tBv)
    _fl = [t for r in tFb for t in r]
    for t in _al:
        t.alias = tuple(_fl)
    for t in _fl:
        t.alias = tuple(_al)
    BBC = U[:, 12288:15360].rearrange("p (k n) -> p k n", k=3)
    BBS = U[:, 15360:16384].rearrange("p (g n) -> p g n", g=8)
    WST = ubf(16384, 16896).rearrange("p (g n) -> p g n", g=8)
    tBC = Tok("bconst")

    def mixer_b(l, b, der, tder, ctx_out):
        DMA("act", BBS.rearrange("p g n -> p (g n)"), d_bws.rearrange("p g n -> p (g n)"), [], [tBC], "bc")
        CP("dve", WST.rearrange("p g n -> p (g n)"), BBS.rearrange("p g n -> p (g n)"), [tBC], [tBC])
        DMA("act", BBS.rearrange("p g n -> p (g n)"), d_bbs.rearrange("p g n -> p (g n)"), [tBC], [tBC], "bc")
        for k3 in range(3):
            DMA("act", BBC[:, k3, :], d_bbc[k3], [], [tBC], "bc")
        groups = [[0, 1], [2, 3]] + ([[4]] if ctx_out else [])
        for G in groups:
            offs = []
            o = 0
            for ti in G:
                offs.append(o)
                o += TILES[ti][1]
            ntok = o
            nblk = ntok // 128
            prenorm(G, BH, [[tBH[c][k] for k in range(len(G))] for c in range(DC)], offs, der, tder, 0, l, 0, b, NB)
            for s in range(DC):
                wv, twv = wload(d_bw[8 + s])
                for k, ti in enumerate(G):
                    n = TILES[ti][1]
                    o = offs[k]
                    nb_ = n // 128
                    ps, tps = rMAIN.next()
                    for bi in range(nb_):
                        mm(ps[:, bi * 128:(bi + 1) * 128], tps, [(BH[:, c, o + bi * 128:o + (bi + 1) * 128], wv[:, c, :]) for c in range(DC)],
                           reads=[twv] + [tBH[c][k] for c in range(DC)])
                    tmp, ttmp = rTMP.next()
                    b0 = o // 128
                    TT("dve", tmp[:, 0:n].rearrange("p (b n) -> p b n", b=nb_), ps[:, 0:n].rearrange("p (b n) -> p b n", b=nb_),
                       BBC[:, 0:1, s * 128:(s + 1) * 128].to_broadcast([128, nb_, 128]), ALU.add, [tps, tBC], [ttmp])
                    ACT(Bv[:, b0:b0 + nb_, s * 128:(s + 1) * 128], tmp[:, 0:n].rearrange("p (b n) -> p b n", b=nb_), AF.Gelu,
                        [ttmp], [tBv[b0 + i] for i in range(nb_)])
            for bi in range(nblk):
                st = LNS
                tst = tLNS
                junk, tjunk = rTMP.next()
                junk2, tjunk2 = rTMP.next()
                jb = junk[:, :].bitcast(BF16)
                jb2 = junk2[:, :].bitcast(BF16)
                ACT(jb, Bv[:, bi, :], AF.Copy, [tBv[bi]], [tjunk, tst], accum_out=st[:, 0:1])
                ACT(jb2, Bv[:, bi, :], AF.Square, [tBv[bi]], [tjunk2, tst], accum_out=st[:, 1:2])
                TS("dve", st[:, 2:3], st[:, 0:1], 1.0 / 1024, None, ALU.mult, ALU.bypass, [tst], [tst])
                TT("dve", st[:, 3:4], st[:, 2:3], st[:, 2:3], ALU.mult, [tst], [tst])
                STT("dve", st[:, 4:5], st[:, 1:2], 1.0 / 1024, st[:, 3:4], ALU.mult, ALU.subtract, [tst], [tst])
                ACT(st[:, 5:6], st[:, 4:5], AF.Sqrt, [tst, tC], [tst], scale=1.0, bias=EPS[:, 1:2])
                RECIP(st[:, 6:7], st[:, 5:6], [tst], [tst])
                for hh in range(2):
                    cs = slice(hh * 512, (hh + 1) * 512)
                    t1, tt1 = rTMP.next()
                    TS("dve", t1[:, :], Bv[:, bi, cs], st[:, 2:3], st[:, 6:7], ALU.subtract, ALU.mult, [tBv[bi], tst], [tt1])
                    TT("pool", t1[:, :], t1[:, :], BBC[:, 1, cs], ALU.mult, [tt1, tBC], [tt1])
                    TT("pool", Bv[:, bi, cs], t1[:, :], BBC[:, 2, cs], ALU.add, [tt1, tBC], [tBv[bi]])
            for j in range(DC):
                wu, twu = wload(d_bw[j])
                for k, ti in enumerate(G):
                    n = TILES[ti][1]
                    o = offs[k]
                    ps, tps = rMAIN.next()
                    mm(ps[:, 0:n], tps, [(wu[:, c, :], BH[:, c, o:o + n]) for c in range(DC)], reads=[twu] + [tBH[c][k] for c in range(DC)])
                    ACT(BuT[:, j, o:o + n], ps[:, 0:n], AF.Gelu, [tps, tC], [tBu[j][k]], bias=BBU[:, j:j + 1])
            for k, ti in enumerate(G):
                n = TILES[ti][1]
                o = offs[k]
                nb_ = n // 128
                b0 = o // 128
                for g in range(DC):
                    ps, tps = rMAIN.next()

                    def emit(e, ps=ps, g=g, b0=b0, nb_=nb_):
                        return [e.matmul(ps[:, bi * 128:(bi + 1) * 128], Bv[:, b0 + bi, g * 128:(g + 1) * 128], WST[:, g, :], start=True, stop=True)
                                for bi in range(nb_)]
                    P.op("pe", emit, reads=[tBv[b0 + i] for i in range(nb_)] + [tBC], writes=[tps])
                    tmp, ttmp = rTMP.next()
                    TT("dve", tmp[:, 0:n].rearrange("p (b n) -> p b n", b=nb_), ps[:, 0:n].rearrange("p (b n) -> p b n", b=nb_),
                       BBS[:, g:g + 1, :].to_broadcast([128, nb_, 128]), ALU.add, [tps, tBC], [ttmp])
                    TT("pool", BuT[:, g, o:o + n], tmp[:, 0:n], BuT[:, g, o:o + n], ALU.mult, [ttmp, tBu[g][k]], [tBu[g][k]])
            Gl = list(G)
            out_proj(lambda j: d_bw[16 + j], lambda c, ti, Gl=Gl, offs=offs: BuT[:, c, offs[Gl.index(ti)]:offs[Gl.index(ti)] + TILES[ti][1]],
                     lambda c, ti, Gl=Gl: tBu[c][Gl.index(ti)], [G], der, tder, 4096, tFb)

    def mixer(l, b, der, tder, last):
        kind = l % 3
        if kind == 0:
            mixer_a(l, b, der, tder, not last)
        elif kind == 1:
            mixer_b(l, b, der, tder, not last)
        else:
            mixer_c(l, b, der, tder, not last)

    setup()
    for b in range(NB):
        P.dma("act", lambda e, b=b: [e.dma_start(out=X[:, c, :], in_=d_xT[b, :, c, :]) for c in range(DC)],
              writes=[t for r in tX for t in r], key="xin", n=DC)
        for l in range(NL):
            last = (l == DEPTH - 1)
            der, tder = derive(l, b, l % 2)
            if do_mixer:
                P.barrier()
                mixer(l, b, der, tder, last)
            if do_ffn:
                P.barrier()
                ffn(l, b, der, tder, not last)
        for ti in range(4):
            s0, n = TILES[ti]
            DMA("act", d_out[b, :, :, s0:s0 + n], X[:, :, s0:s0 + n], [tX[c][ti] for c in range(DC)], [tOUT], "out")
    P.op("act", lambda e: e.nop(), reads=[tOUT])
    P.build(nc)
    es.close()
    return nc


def _slabs(W, cols_list):
    din = W.shape[0]
    C = din // 128
    out = np.empty((len(cols_list), 128, C, 128), np.float32)
    Wr = W.reshape(C, 128, W.shape[1])
    for s, cols in enumerate(cols_list):
        out[s] = Wr[:, :, cols].transpose(1, 0, 2)
    return out


def _rope_tables(head_dim):
    GRID_W = 64
    rows = NLAT // GRID_W
    row_pos = np.repeat(np.arange(rows, dtype=np.float32), GRID_W)
    col_pos = np.tile(np.arange(GRID_W, dtype=np.float32), rows)
    n_freq = head_dim // 4
    inv_freq = (10000.0 ** (-np.arange(n_freq, dtype=np.float32) / n_freq)).astype(np.float32)
    ang = np.concatenate([row_pos[:, None] * inv_freq, col_pos[:, None] * inv_freq], -1)
    half = head_dim // 2
    p = np.arange(128)
    idx = p % half
    sign = np.where((p % head_dim) < half, -1.0, 1.0).astype(np.float32)
    cos = np.cos(ang)[:, idx].T.astype(np.float32)
    sin = (np.sin(ang)[:, idx].T * sign[:, None]).astype(np.float32)
    return np.stack([cos, sin], 0)


def _perm(head_dim):
    half = head_dim // 2
    m = np.zeros((128, 128), np.float32)
    for mm_ in range(128):
        partner = mm_ + half if (mm_ % head_dim) < half else mm_ - half
        m[partner, mm_] = 1.0
    return m


def a_head_of(j, hf):
    gp, i = j // 4, j % 4
    return 8 * gp + 4 * hf + i


def prep_shared(inp):
    f = lambda k: np.asarray(inp[k], np.float32)
    sh = {}
    ada_w = f("ada_w")
    sh["adaw"] = np.stack([_slabs(ada_w[l], [np.arange(j * 128, (j + 1) * 128) for j in range(48)]) for l in range(DEPTH)], 0)
    sh["adab"] = np.ascontiguousarray(f("ada_b").reshape(DEPTH, 48, 128).transpose(2, 0, 1))
    sh["ng"] = np.ascontiguousarray(f("norm_g").reshape(DEPTH, 4, DC, 128).transpose(3, 0, 1, 2))
    win = f("ffn_w_in")
    cols = []
    for hc in range(HC):
        cols.append(np.arange(hc * 128, (hc + 1) * 128))
        cols.append(np.arange(FH + hc * 128, FH + (hc + 1) * 128))
    sh["fin"] = np.stack([_slabs(win[l], cols) for l in range(DEPTH)], 0)
    wout = f("ffn_w_out")
    fo = np.zeros((DEPTH, DC, 3, 128, 8, 128), np.float32)
    for l in range(DEPTH):
        Wr = wout[l].reshape(HC, 128, D)
        for j in range(DC):
            blk = Wr[:, :, j * 128:(j + 1) * 128].transpose(1, 0, 2)
            fo[l, j, 0] = blk[:, 0:8]
            fo[l, j, 1] = blk[:, 8:16]
            fo[l, j, 2, :, 0:6] = blk[:, 16:22]
    sh["fout"] = fo
    aq, ao, asink = f("a_w_qkv"), f("a_w_o"), f("a_sink")
    aw = np.zeros((2, 20, 128, DC, 128), np.float32)
    ask = np.zeros((128, 2, DC), np.float32)
    for la in range(2):
        qcols = []
        for j in range(DC):
            qcols.append(np.concatenate([np.arange(a_head_of(j, hf) * 64, a_head_of(j, hf) * 64 + 64) for hf in range(2)]))
        kcols = [np.arange(1024 + g * 128, 1024 + (g + 1) * 128) for g in range(2)]
        vcols = [np.arange(1280 + g * 128, 1280 + (g + 1) * 128) for g in range(2)]
        aw[la, 0:12] = _slabs(aq[la], qcols + kcols + vcols)
        rows = np.concatenate(qcols)
        wo_p = ao[la][rows, :]
        aw[la, 12:20] = _slabs(wo_p, [np.arange(j * 128, (j + 1) * 128) for j in range(DC)])
        for j in range(DC):
            for hf in range(2):
                ask[hf * 64:(hf + 1) * 64, la, j] = asink[la, a_head_of(j, hf)]
    sh["aw"] = aw
    sh["asink"] = ask
    bwin, bwo = f("b_w_in")[0], f("b_w_o")[0]
    sh["bw"] = np.concatenate([_slabs(bwin, [np.arange(j * 128, (j + 1) * 128) for j in range(16)]),
                               _slabs(bwo, [np.arange(j * 128, (j + 1) * 128) for j in range(8)])], 0)
    bbin = f("b_b_in")[0]
    sh["bbu"] = np.ascontiguousarray(bbin[:1024].reshape(DC, 128).T)
    sh["bbc"] = np.stack([np.broadcast_to(bbin[1024:], (128, 1024)), np.broadcast_to(f("b_ln_g")[0], (128, 1024)),
                          np.broadcast_to(f("b_ln_b")[0], (128, 1024))], 0).astype(np.float32)
    sh["bws"] = np.ascontiguousarray(f("b_w_s")[0].transpose(2, 0, 1))
    sh["bbs"] = np.ascontiguousarray(np.broadcast_to(f("b_b_s")[0][None], (128, 8, 128))).astype(np.float32)
    cq, co = f("c_w_qkv")[0], f("c_w_o")[0]
    sh["cw"] = np.concatenate([_slabs(cq, [np.arange(j * 128, (j + 1) * 128) for j in range(16)]),
                               _slabs(co, [np.arange(j * 128, (j + 1) * 128) for j in range(8)])], 0)
    sh["cg"] = np.stack([f("c_q_g")[0], f("c_k_g")[0]], 1).astype(np.float32)
    sh["rope"] = np.stack([_rope_tables(64), _rope_tables(128)], 0)
    ident = np.eye(128, dtype=np.float32)
    k = np.arange(128)[:, None]
    q = np.arange(128)[None, :]
    mnext = np.where(k <= q, 0.0, -30000.0).astype(np.float32)
    mprev = np.where(q <= k, 0.0, -30000.0).astype(np.float32)
    sh["cst"] = np.stack([_perm(64), _perm(128), ident, mnext, mprev], 0)
    return sh


def prep_core(inp, bs):
    x = np.asarray(inp["x"], np.float32)[bs]
    ctx = np.asarray(inp["ctx"], np.float32)[bs]
    nb = x.shape[0]
    xc = np.concatenate([x, ctx], 1)
    xT = np.ascontiguousarray(xc.reshape(nb, T, DC, 128).transpose(0, 3, 2, 1))
    c = np.asarray(inp["c"], np.float32)[bs]
    cc = np.concatenate([c, np.asarray(inp["c_ctx"], np.float32)[None]], 0)
    cT = np.ascontiguousarray(cc.reshape(nb + 1, DC, 128).transpose(2, 1, 0))
    return {"xT": xT, "cT": cT}


_NC_CACHE = {}


def kernel(**inputs):
    NB = 4
    key = ("full", NB)
    if key not in _NC_CACHE:
        _NC_CACHE[key] = build_program(NB=NB)
    nc = _NC_CACHE[key]
    sh = prep_shared(inputs)
    in_maps = []
    for i in range(NCORES):
        m = dict(sh)
        m.update(prep_core(inputs, slice(i * NB, (i + 1) * NB)))
        in_maps.append(m)
    res = run_bass_kernel_spmd(nc, in_maps, core_ids=list(range(NCORES)))
    outs = []
    for i in range(NCORES):
        oT = np.asarray(res.results[i]["outT"])
        outs.append(oT.transpose(0, 3, 2, 1).reshape(NB, NLAT, D))
    return np.ascontiguousarray(np.concatenate(outs, 0)).astype(np.float32)
```

```python
from contextlib import ExitStack
import math
import numpy as np
import concourse.bass as bass
import concourse.mybir as mybir
from concourse.bass_utils import run_bass_kernel_spmd

F32 = mybir.dt.float32
BF16 = mybir.dt.bfloat16
AF = mybir.ActivationFunctionType
ALU = mybir.AluOpType

D = 1024
DC = 8
NLAT = 2048
NCTX = 256
T = NLAT + NCTX
DEPTH = 4
FH = 2816
HC = 22
RMS_EPS = 1e-6
LN_EPS = 1e-5
TILES = [(0, 512), (512, 512), (1024, 512), (1536, 512), (2048, 256)]
NCORES = 8


class Tok:
    __slots__ = ("name", "last_w", "readers", "alias")

    def __init__(self, name):
        self.name = name
        self.last_w = None
        self.readers = []
        self.alias = ()


class Op:
    __slots__ = ("eng", "emit", "dma", "key", "n", "deps", "signal", "sem", "val", "waits", "pos")


SEM_ROT = 30000


class Prog:
    def __init__(self):
        self.ops = []
        self.per_eng = {}

    def _add(self, eng, emit, reads, writes, dma=False, key=None, n=1):
        op = Op()
        op.eng = eng
        op.emit = emit
        op.dma = dma
        op.key = key
        op.n = n
        op.signal = False
        op.sem = None
        op.val = 0
        lst = self.per_eng.setdefault(eng, [])
        op.pos = len(lst)
        lst.append(op)
        deps = set()
        rs = []
        for t in reads:
            rs.append(t)
            rs.extend(t.alias)
        ws = []
        for t in writes:
            ws.append(t)
            ws.extend(t.alias)
        for t in rs:
            if t.last_w is not None:
                deps.add(t.last_w)
        for t in ws:
            if t.last_w is not None:
                deps.add(t.last_w)
            deps.update(t.readers)
        for t in rs:
            t.readers.append(op)
        for t in ws:
            t.last_w = op
            t.readers = []
        deps.discard(op)
        fin = []
        for d in deps:
            if (not d.dma) and (not dma) and d.eng == eng:
                if eng == "pe":
                    continue
            d.signal = True
            fin.append(d)
        op.deps = fin
        self.ops.append(op)
        return op

    def op(self, eng, emit, reads=(), writes=()):
        return self._add(eng, emit, reads, writes)

    def dma(self, queue, emit, reads=(), writes=(), key="dma", n=1):
        return self._add(queue, emit, reads, writes, dma=True, key=key, n=n)

    def barrier(self, engs=("pe", "act", "dve", "pool")):
        bt = [Tok("bar") for _ in engs]
        for e, t in zip(engs, bt):
            self.op(e, lambda x: x.drain(), writes=[t])
        for e in engs:
            self.op(e, lambda x: x.nop(), reads=bt)

    def build(self, nc):
        stack = ExitStack()
        nsem = [0]

        def newsem(name):
            nsem[0] += 1
            return stack.enter_context(nc.semaphore("%s_%d" % (name, nsem[0])))

        cur = {}
        cnt = {}
        for op in self.ops:
            if not op.signal:
                continue
            k = ("d", op.key) if op.dma else ("e", op.eng)
            inc = 16 * op.n if op.dma else 1
            if k not in cur or cnt[k] + inc > SEM_ROT:
                cur[k] = newsem("s" + str(k[1]))
                cnt[k] = 0
            cnt[k] += inc
            op.sem = cur[k]
            op.val = cnt[k]
        seen = {}
        for op in self.ops:
            sd = seen.setdefault(op.eng, {})
            need = {}
            for d in op.deps:
                sid = id(d.sem)
                if sd.get(sid, 0) >= d.val:
                    continue
                if sid not in need or need[sid][1] < d.val:
                    need[sid] = (d.sem, d.val)
            op.waits = list(need.values())
            for sid, (s, v) in need.items():
                sd[sid] = v
        self.nsem = nsem[0]
        engs = self.per_eng
        with stack:
            with nc.Block() as block:
                def run(e, name):
                    for op in engs.get(name, []):
                        for (s, v) in op.waits:
                            e.wait_ge(s, v)
                        ins = op.emit(e)
                        if not isinstance(ins, (list, tuple)):
                            ins = [ins]
                        if op.signal:
                            if op.dma:
                                assert len(ins) == op.n, (len(ins), op.n)
                                for i in ins:
                                    i.then_inc(op.sem, 16)
                            else:
                                ins[-1].then_inc(op.sem, 1)

                @block.tensor
                def _(e):
                    run(e, "pe")

                @block.scalar
                def _(e):
                    run(e, "act")

                @block.vector
                def _(e):
                    run(e, "dve")

                @block.gpsimd
                def _(e):
                    run(e, "pool")

                @block.sync
                def _(e):
                    run(e, "sp")


class Rot:
    def __init__(self, items):
        self.items = items
        self.i = 0

    def next(self):
        r = self.items[self.i % len(self.items)]
        self.i += 1
        return r


def build_program(NB=4, NL=DEPTH, do_mixer=True, do_ffn=True):
    nc = bass.Bass("TRN2", target_bir_lowering=False)
    P = Prog()
    es = ExitStack()

    def dram(name, shape, kind="ExternalInput", dt=F32):
        return nc.dram_tensor(name, list(shape), dt, kind=kind).ap()

    def sb(name, shape, dt):
        return es.enter_context(nc.sbuf_tensor(name, list(shape), dt))

    d_xT = dram("xT", [NB, 128, DC, T])
    d_cT = dram("cT", [128, DC, NB + 1])
    d_adaw = dram("adaw", [DEPTH, 48, 128, DC, 128])
    d_adab = dram("adab", [128, DEPTH, 48])
    d_ng = dram("ng", [128, DEPTH, 4, DC])
    d_fin = dram("fin", [DEPTH, 2 * HC, 128, DC, 128])
    d_fout = dram("fout", [DEPTH, DC, 3, 128, 8, 128])
    d_aw = dram("aw", [2, 20, 128, DC, 128])
    d_asink = dram("asink", [128, 2, DC])
    d_bw = dram("bw", [24, 128, DC, 128])
    d_bbu = dram("bbu", [128, DC])
    d_bbc = dram("bbc", [3, 128, 1024])
    d_bws = dram("bws", [128, 8, 128])
    d_bbs = dram("bbs", [128, 8, 128])
    d_cw = dram("cw", [24, 128, DC, 128])
    d_cg = dram("cg", [128, 2])
    d_rope = dram("rope", [2, 2, 128, NLAT])
    d_cst = dram("cst", [5, 128, 128])
    d_out = dram("outT", [NB, 128, DC, NLAT], kind="ExternalOutput")

    NWB = 4
    X = sb("X", [128, DC, T], F32)
    U = sb("U", [128, 24320], F32)
    WS = [sb("WS%d" % i, [128, 1024], F32) for i in range(2)]
    WB = [sb("WB%d" % i, [128, 8, 128], BF16) for i in range(NWB)]
    TMP = [sb("TMP%d" % i, [128, 512], F32) for i in range(3)]
    RS = [sb("RS%d" % i, [128, 512], F32) for i in range(2)]
    SQ = [sb("SQ%d" % i, [128, 512], BF16) for i in range(2)]
    PT = [sb("PT%d" % i, [128, 512], BF16) for i in range(3)]
    MOD = sb("MOD", [128, DEPTH, 48, NB + 1], F32)
    DER = [sb("DER%d" % i, [128, 2, 4, DC], F32) for i in range(2)]
    NG = sb("NG", [128, DEPTH, 4, DC], F32)
    ADAB = sb("ADAB", [128, DEPTH, 48], F32)
    CST32 = sb("CST32", [128, 128], F32)
    CST = sb("CST", [128, 6, 128], BF16)
    SC = sb("SC", [128, DC, NB + 1], BF16)
    C32 = sb("C32", [128, DC, NB + 1], F32)
    EPS = sb("EPS", [128, 4], F32)
    ESINK = sb("ESINK", [128, 2, DC], F32)
    BBU = sb("BBU", [128, DC], F32)
    CG = sb("CG", [128, 2], F32)
    LNS = sb("LNS", [128, 8, 8], F32)
    PS = [es.enter_context(nc.psum_tensor("PS%d" % i, [128, 512], F32)) for i in range(8)]

    def toks(name, *dims):
        if len(dims) == 1:
            return [Tok("%s%d" % (name, i)) for i in range(dims[0])]
        return [toks("%s%d_" % (name, i), *dims[1:]) for i in range(dims[0])]

    tX = toks("X", DC, 5)
    tWS = toks("WS", 2)
    tWB = toks("WB", NWB)
    tPS = toks("PS", 8)
    rWS = Rot(list(zip(WS, tWS, range(2))))
    rWB = Rot(list(zip(WB, tWB)))
    rTMP = Rot(list(zip(TMP, toks("TMP", 3))))
    rRS = Rot(list(zip(RS, toks("RS", 2))))
    rSQ = Rot(list(zip(SQ, toks("SQ", 2))))
    rPT = Rot(list(zip(PT, toks("PT", 3))))
    rMAIN = Rot([(PS[i], tPS[i]) for i in range(4)])
    rAUX = Rot([(PS[i], tPS[i]) for i in range(4, 7)])
    tROPE = Tok("ROPE")
    tMOD = toks("MOD", DEPTH)
    tLNS = toks("LNS", 8)
    tDER = toks("DER", 2)
    tC = Tok("consts")
    tOUT = Tok("out")
    ones = CST[:, 0, :]
    permA = CST[:, 1, :]
    permC = CST[:, 2, :]
    ident = CST[:, 3, :]
    mask_next = CST[:, 4, :]
    mask_prev = CST[:, 5, :]

    def ubf(a, b, **kw):
        v = U[:, a:b].bitcast(BF16)
        return v

    H = ubf(0, 9216).rearrange("p (c t) -> p c t", c=DC)
    tH = toks("H", DC, 5)
    QO = ubf(9216, 18432).rearrange("p (c t) -> p c t", c=DC)
    tQO = toks("QO", DC, 5)
    KT = ubf(18432, 20736).rearrange("p (c t) -> p c t", c=2)
    tKT = toks("KT", 2, 5)
    VV = ubf(20736, 23040).rearrange("p (b n) -> p b n", b=18)
    tVV = toks("VV", 18)
    ROPE = U[:, 23040:24064].rearrange("p (k n) -> p k n", k=2)
    GN = 1280
    Fst = U[:, 0:10240].rearrange("p (c t) -> p c t", c=DC)
    tF = toks("F", DC, 3)
    Hg = ubf(0, 5120).rearrange("p (c t) -> p c t", c=DC)
    tHg = toks("Hg", DC, 3)
    Hid = ubf(10240, 24320).rearrange("p (c t) -> p c t", c=HC)
    tHid = toks("Hid", HC, 3)
    allHg = [t for r in tHg for t in r]
    allF = [t for r in tF for t in r]
    for t in allHg:
        t.alias = tuple(allF)
    for t in allF:
        t.alias = tuple(allHg)
    uT = ubf(9216, 14336).rearrange("p (c t) -> p c t", c=DC)
    tuT = toks("uT", DC, 3)
    vtm = ubf(14336, 19456).rearrange("p (b n) -> p b n", b=10)
    tvtm = toks("vtm", 10)
    BBC = U[:, 19456:22528].rearrange("p (k n) -> p k n", k=3)
    BBS = U[:, 22528:23552].rearrange("p (g n) -> p g n", g=8)
    WST = ubf(23552, 24064).rearrange("p (g n) -> p g n", g=8)
    tBC = Tok("bconst")

    def ACT(out, in_, func, reads, writes, **kw):
        P.op("act", lambda e: e.activation(out=out, in_=in_, func=func, **kw), reads=reads, writes=writes)

    def TT(eng, out, in0, in1, op, reads, writes):
        P.op(eng, lambda e: e.tensor_tensor(out=out, in0=in0, in1=in1, op=op), reads=reads, writes=writes)

    def TS(eng, out, in0, s1, s2, op0, op1, reads, writes):
        P.op(eng, lambda e: e.tensor_scalar(out=out, in0=in0, scalar1=s1, scalar2=s2, op0=op0, op1=op1), reads=reads, writes=writes)

    def STT(eng, out, in0, scalar, in1, op0, op1, reads, writes):
        P.op(eng, lambda e: e.scalar_tensor_tensor(out=out, in0=in0, scalar=scalar, in1=in1, op0=op0, op1=op1), reads=reads, writes=writes)

    def CP(eng, out, in_, reads, writes):
        P.op(eng, lambda e: e.tensor_copy(out=out, in_=in_), reads=reads, writes=writes)

    def RECIP(out, in_, reads, writes):
        P.op("dve", lambda e: e.reciprocal(out=out, in_=in_), reads=reads, writes=writes)

    def MM1(out, lhsT, rhs, start, stop, reads, writes):
        P.op("pe", lambda e: e.matmul(out, lhsT, rhs, start=start, stop=stop), reads=reads, writes=writes)

    def DMA(queue, out, in_, reads, writes, key):
        P.dma(queue, lambda e: [e.dma_start(out=out, in_=in_)], reads=reads, writes=writes, key=key)

    def wload(src, C=8, eng="pool"):
        ws, tws, wi = rWS.next()
        wb, twb = rWB.next()
        n = C * 128
        DMA("sp", ws[:, 0:n], src.rearrange("p c n -> p (c n)"), [], [tws], "ws%d" % wi)
        if eng == "act":
            ACT(wb[:, 0:C, :].rearrange("p c n -> p (c n)"), ws[:, 0:n], AF.Copy, [tws], [twb])
        else:
            CP(eng, wb[:, 0:C, :].rearrange("p c n -> p (c n)"), ws[:, 0:n], [tws], [twb])
        return wb, twb

    def mm(ps, tps, pairs, reads):
        pairs = list(pairs)

        def emit(e):
            r = []
            k = len(pairs)
            for i, (l, rr) in enumerate(pairs):
                r.append(e.matmul(ps, l, rr, start=(i == 0), stop=(i == k - 1)))
            return r
        P.op("pe", emit, reads=reads, writes=[tps])

    def rstd_from_ps(ps, tps, n, scale, eps_col):
        rs, trs = rRS.next()
        ACT(rs[:, 0:n], ps[:, 0:n], AF.Sqrt, [tps, tC], [trs], scale=scale, bias=EPS[:, eps_col:eps_col + 1])
        RECIP(rs[:, 0:n], rs[:, 0:n], [trs], [trs])
        return rs, trs

    def prenorm(ti_list, dst, tdst, dst_off, der, tder, kind_a, l, shift_k, col_lat, col_ctx):
        for k, ti in enumerate(ti_list):
            s0, n = TILES[ti]
            lc = 1 if ti == 4 else 0
            col = col_ctx if ti == 4 else col_lat
            ps, tps = rAUX.next()
            for c in range(DC):
                sq, tsq = rSQ.next()
                if c % 2 == 0:
                    ACT(sq[:, 0:n], X[:, c, s0:s0 + n], AF.Square, [tX[c][ti]], [tsq])
                else:
                    TT("dve", sq[:, 0:n], X[:, c, s0:s0 + n], X[:, c, s0:s0 + n], ALU.mult, [tX[c][ti]], [tsq])
                MM1(ps[:, 0:n], ones, sq[:, 0:n], c == 0, c == DC - 1, [tsq, tC], [tps])
            rs, trs = rstd_from_ps(ps, tps, n, 1.0 / D, 0)
            o = dst_off[k]
            for c in range(DC):
                tmp, ttmp = rTMP.next()
                STT("dve", tmp[:, 0:n], X[:, c, s0:s0 + n], der[:, lc, kind_a, c:c + 1], rs[:, 0:n], ALU.mult, ALU.mult,
                    [tX[c][ti], trs, tder], [ttmp])
                ACT(dst[:, c, o:o + n], tmp[:, 0:n], AF.Identity, [ttmp, tMOD[l]], [tdst[c][k]],
                    bias=MOD[:, l, shift_k * 8 + c, col:col + 1])

    def postnorm_stage(ps, tps, j, n, fdst, tfd, ss, tss, gam, tgam):
        ACT(fdst, ps[:, 0:n], AF.Copy, [tps, tgam], [tfd], scale=gam)
        sq, tsq = rSQ.next()
        ACT(sq[:, 0:n], ps[:, 0:n], AF.Square, [tps], [tsq])
        MM1(ss[:, 0:n], ones, sq[:, 0:n], j == 0, j == DC - 1, [tsq, tC], [tss])

    def postnorm_apply(ti, n, fsrc, tfs, ss, tss, der, tder, kind_g):
        s0, _ = TILES[ti]
        lc = 1 if ti == 4 else 0
        rs, trs = rstd_from_ps(ss, tss, n, 1.0 / D, 0)
        for j in range(DC):
            TT("dve", fsrc[j], fsrc[j], rs[:, 0:n], ALU.mult, [tfs[j], trs], [tfs[j]])
        for j in range(DC):
            TT("dve", X[:, j, s0:s0 + n], X[:, j, s0:s0 + n], fsrc[j], ALU.add, [tfs[j], tX[j][ti]], [tX[j][ti]])

    def setup():
        P.dma("act", lambda e: [e.dma_start(out=NG[:], in_=d_ng), e.dma_start(out=ADAB[:], in_=d_adab),
                                e.dma_start(out=C32[:], in_=d_cT), e.dma_start(out=ESINK[:], in_=d_asink),
                                e.dma_start(out=BBU[:], in_=d_bbu), e.dma_start(out=CG[:], in_=d_cg)],
              writes=[tC], key="c0", n=6)
        P.op("dve", lambda e: [e.memset(EPS[:, 0:1], RMS_EPS), e.memset(EPS[:, 1:2], LN_EPS), e.memset(CST[:, 0, :], 1.0)],
             writes=[tC])
        tc32 = Tok("cst32")
        for k in range(5):
            DMA("act", CST32[:], d_cst[k], [], [tc32], "c1")
            CP("dve", CST[:, k + 1, :], CST32[:], [tc32], [tC])
        ACT(SC[:], C32[:], AF.Silu, [tC], [tC])
        ACT(ESINK[:], ESINK[:], AF.Exp, [tC], [tC])
        for l in range(NL):
            for j in range(48):
                wb, twb = wload(d_adaw[l, j], eng=("pool", "dve", "act", "dve")[j % 4])
                ps, tps = rMAIN.next()
                mm(ps[:, 0:NB + 1], tps, [(wb[:, c, :], SC[:, c, :]) for c in range(DC)], reads=[twb, tC])
                ACT(MOD[:, l, j, :], ps[:, 0:NB + 1], AF.Identity, [tps, tC], [tMOD[l]], bias=ADAB[:, l, j:j + 1])

    def derive(l, b, slot):
        der, tder = DER[slot], tDER[slot]
        for lc, col in ((0, b), (1, NB)):
            for (k, sck, ngk, isgate) in ((0, 1, 0, False), (1, 2, 1, True), (2, 4, 2, False), (3, 5, 3, True)):
                src = MOD[:, l, sck * 8:(sck + 1) * 8, col]
                if isgate:
                    TT("dve", der[:, lc, k, :], src, NG[:, l, ngk, :], ALU.mult, [tMOD[l], tC], [tder])
                else:
                    STT("dve", der[:, lc, k, :], src, 1.0, NG[:, l, ngk, :], ALU.add, ALU.mult, [tMOD[l], tC], [tder])
        return der, tder

    def ffn(l, b, der, tder, with_ctx):
        groups = [[0, 1], [2, 3, 4] if with_ctx else [2, 3]]
        for G in groups:
            offs = []
            o = 0
            for ti in G:
                offs.append(o)
                o += TILES[ti][1]
            tHgG = [[tHg[c][k] for k in range(len(G))] for c in range(DC)]
            prenorm(G, Hg, tHgG, offs, der, tder, 2, l, 3, b, NB)
            nxt = (wload(d_fin[l, 0]), wload(d_fin[l, 1], eng="dve"))
            for hc in range(HC):
                (wg, twg), (wu, twu) = nxt
                if hc + 1 < HC:
                    nxt = (wload(d_fin[l, 2 * hc + 2]), wload(d_fin[l, 2 * hc + 3], eng="dve"))
                for k, ti in enumerate(G):
                    n = TILES[ti][1]
                    o = offs[k]
                    rd = [tHg[c][k] for c in range(DC)]
                    pg, tpg = rMAIN.next()
                    mm(pg[:, 0:n], tpg, [(wg[:, c, :], Hg[:, c, o:o + n]) for c in range(DC)], reads=[twg] + rd)
                    pu, tpu = rMAIN.next()
                    mm(pu[:, 0:n], tpu, [(wu[:, c, :], Hg[:, c, o:o + n]) for c in range(DC)], reads=[twu] + rd)
                    tmp, ttmp = rTMP.next()
                    ACT(tmp[:, 0:n], pg[:, 0:n], AF.Silu, [tpg], [ttmp])
                    TT("dve", Hid[:, hc, o:o + n], tmp[:, 0:n], pu[:, 0:n], ALU.mult, [ttmp, tpu], [tHid[hc][k]])
            ssb = [(PS[4 + k], tPS[4 + k]) for k in range(len(G))]
            for j in range(DC):
                parts = [wload(d_fout[l, j, pt, :, 0:(8 if pt < 2 else 6), :], C=(8 if pt < 2 else 6), eng=("dve" if pt == 1 else "pool")) for pt in range(3)]
                for k, ti in enumerate(G):
                    n = TILES[ti][1]
                    o = offs[k]
                    py, tpy = rMAIN.next()
                    mm(py[:, 0:n], tpy, [(parts[hc // 8][0][:, hc % 8, :], Hid[:, hc, o:o + n]) for hc in range(HC)],
                       reads=[p[1] for p in parts] + [tHid[hc][k] for hc in range(HC)])
                    postnorm_stage(py, tpy, j, n, Fst[:, j, o:o + n], tF[j][k], ssb[k][0], ssb[k][1],
                                   der[:, 1 if ti == 4 else 0, 3, j:j + 1], tder)
            for k, ti in enumerate(G):
                n = TILES[ti][1]
                o = offs[k]
                postnorm_apply(ti, n, [Fst[:, j, o:o + n] for j in range(DC)], [tF[j][k] for j in range(DC)],
                               ssb[k][0], ssb[k][1], der, tder, 3)

    tFm = toks("Fm", DC, 2)
    rACC = Rot([((PS[4], tPS[4]), (PS[5], tPS[5])), ((PS[6], tPS[6]), (PS[7], tPS[7]))])
    EXP_A = 0.125
    EXP_C = 128.0 ** -0.5

    def out_proj(wsrc, O, tO, groups, der, tder, Fbase, tFx, nslot=1):
        for gi, G in enumerate(groups):
            if nslot == 2:
                assert len(G) == 1
                sl = gi % 2
                Fm = U[:, Fbase + 4096 * sl:Fbase + 4096 * (sl + 1)].rearrange("p (c t) -> p c t", c=DC)
                tFg = [[tFx[j][sl]] for j in range(DC)]
                ssb = [(PS[4 + sl], tPS[4 + sl])]
            else:
                Fm = U[:, Fbase:Fbase + 8192].rearrange("p (c t) -> p c t", c=DC)
                tFg = tFx
                ssb = [(PS[4 + k], tPS[4 + k]) for k in range(len(G))]
            offs = []
            o = 0
            for ti in G:
                offs.append(o)
                o += TILES[ti][1]
            for j in range(DC):
                wo, two = wload(wsrc(j))
                for k, ti in enumerate(G):
                    s0, n = TILES[ti]
                    py, tpy = rMAIN.next()
                    mm(py[:, 0:n], tpy, [(wo[:, c, :], O(c, ti)) for c in range(DC)], reads=[two] + [tO(c, ti) for c in range(DC)])
                    postnorm_stage(py, tpy, j, n, Fm[:, j, offs[k]:offs[k] + n], tFg[j][k], ssb[k][0], ssb[k][1],
                                   der[:, 1 if ti == 4 else 0, 1, j:j + 1], tder)
            for k, ti in enumerate(G):
                n = TILES[ti][1]
                postnorm_apply(ti, n, [Fm[:, j, offs[k]:offs[k] + n] for j in range(DC)], [tFg[j][k] for j in range(DC)],
                               ssb[k][0], ssb[k][1], der, tder, 1)

    def rope_pass(which, chunks):
        perm = permA if which == 0 else permC
        for ti in range(4):
            s0, n = TILES[ti]
            P.dma("act", lambda e, s0=s0, n=n: [e.dma_start(out=ROPE[:, 0, :], in_=d_rope[which, 0, :, s0:s0 + n]),
                                                 e.dma_start(out=ROPE[:, 1, :], in_=d_rope[which, 1, :, s0:s0 + n])],
                  writes=[tROPE], key="rope", n=2)
            for (buf, c, trow) in chunks:
                src = buf[:, c, s0:s0 + n]
                tk = trow[ti]
                ps, tps = rAUX.next()
                MM1(ps[:, 0:n], perm, src, True, True, [tk, tC], [tps])
                t1, tt1 = rTMP.next()
                TT("pool", t1[:, 0:n], src, ROPE[:, 0, :], ALU.mult, [tk, tROPE], [tt1])
                t2, tt2 = rTMP.next()
                TT("dve", t2[:, 0:n], ps[:, 0:n], ROPE[:, 1, :], ALU.mult, [tps, tROPE], [tt2])
                TT("pool", src, t1[:, 0:n], t2[:, 0:n], ALU.add, [tt1, tt2], [tk])

    def v_proj(wlist, tiles, nblk_cols):
        for ti in tiles:
            s0, n = TILES[ti]
            for bi in range(n // 128):
                blk = s0 // 128 + bi
                ps, tps = rMAIN.next()
                for s, (wv, twv) in enumerate(wlist):
                    mm(ps[:, s * 128:(s + 1) * 128], tps, [(H[:, c, s0 + bi * 128:s0 + (bi + 1) * 128], wv[:, c, :]) for c in range(DC)],
                       reads=[twv] + [tH[c][ti] for c in range(DC)])
                CP("dve", VV[:, blk, 0:nblk_cols], ps[:, 0:nblk_cols], [tps], [tVV[blk]])

    def mixer_a(l, b, der, tder, ctx_out):
        la = l // 3
        allt = [0, 1, 2, 3, 4]
        qt = allt if ctx_out else [0, 1, 2, 3]
        prenorm(allt, H, [[tH[c][ti] for ti in allt] for c in range(DC)], [TILES[ti][0] for ti in allt], der, tder, 0, l, 0, b, NB)
        for j in range(DC):
            wq, twq = wload(d_aw[la, j])
            for ti in qt:
                s0, n = TILES[ti]
                ps, tps = rMAIN.next()
                mm(ps[:, 0:n], tps, [(wq[:, c, :], H[:, c, s0:s0 + n]) for c in range(DC)], reads=[twq] + [tH[c][ti] for c in range(DC)])
                ACT(QO[:, j, s0:s0 + n], ps[:, 0:n], AF.Copy, [tps], [tQO[j][ti]])
        for g in range(2):
            wk, twk = wload(d_aw[la, 8 + g])
            for ti in allt:
                s0, n = TILES[ti]
                ps, tps = rMAIN.next()
                mm(ps[:, 0:n], tps, [(wk[:, c, :], H[:, c, s0:s0 + n]) for c in range(DC)], reads=[twk] + [tH[c][ti] for c in range(DC)])
                ACT(KT[:, g, s0:s0 + n], ps[:, 0:n], AF.Copy, [tps], [tKT[g][ti]])
        rope_pass(0, [(QO, j, tQO[j]) for j in range(DC)] + [(KT, g, tKT[g]) for g in range(2)])
        wl = [wload(d_aw[la, 10]), wload(d_aw[la, 11])]
        v_proj(wl, allt, 256)
        for j in range(DC):
            kc = j // 4
            for ti in qt:
                (acc, tacc), (den, tden) = rACC.next()
                s0, n = TILES[ti]
                qb0 = s0 // 128
                if ti < 4:
                    kbs = list(range(max(0, qb0 - 1), min(15, qb0 + 4) + 1)) + [16, 17]
                else:
                    kbs = [16, 17]
                for hf in range(2):
                    g = 2 * kc + hf
                    r0, r1 = hf * 64, (hf + 1) * 64
                    pend = None

                    def pv(item, first):
                        pt, tpt, kb, c0, c1 = item
                        kti = min(kb // 4, 4)
                        def emit(e, pt=pt, kb=kb, c0=c0, c1=c1, first=first, r0=r0, r1=r1, g=g, acc=acc, den=den):
                            return [e.matmul(acc[r0:r1, c0:c1], VV[:, kb, g * 64:(g + 1) * 64], pt[:, c0:c1], start=first, stop=False, skip_group_check=True),
                                    e.matmul(den[r0:r1, c0:c1], ones[:, 0:64], pt[:, c0:c1], start=first, stop=False, skip_group_check=True)]
                        P.op("pe", emit, reads=[tpt, tVV[kb], tC], writes=[tacc, tden])

                    first = True
                    for kb in kbs:
                        if kb < 16 and ti < 4:
                            cb0, cb1 = max(kb - 1, qb0), min(kb + 1, qb0 + 3)
                            c0, c1 = (cb0 - qb0) * 128, (cb1 - qb0 + 1) * 128
                        else:
                            cb0 = cb1 = None
                            c0, c1 = 0, n
                        kti = min(kb // 4, 4)
                        ps, tps = rMAIN.next()
                        pairs = [(ps[:, c0:c1], KT[r0:r1, kc, kb * 128:(kb + 1) * 128], QO[r0:r1, j, s0 + c0:s0 + c1])]
                        if cb0 is not None:
                            if cb0 <= kb - 1 <= cb1:
                                cc = (kb - 1 - qb0) * 128
                                pairs.append((ps[:, cc:cc + 128], ident, mask_next))
                            if cb0 <= kb + 1 <= cb1:
                                cc = (kb + 1 - qb0) * 128
                                pairs.append((ps[:, cc:cc + 128], ident, mask_prev))

                        def emit(e, pairs=pairs):
                            r = []
                            for i, (o_, l_, r_) in enumerate(pairs):
                                r.append(e.matmul(o_, l_, r_, start=(i == 0), stop=(i == len(pairs) - 1)))
                            return r
                        P.op("pe", emit, reads=[tKT[kc][kti], tQO[j][ti], tC], writes=[tps])
                        pt, tpt = rPT.next()
                        ACT(pt[:, c0:c1], ps[:, c0:c1], AF.Exp, [tps], [tpt], scale=EXP_A)
                        if pend is not None:
                            pv(pend, first)
                            first = False
                        pend = (pt, tpt, kb, c0, c1)
                    pv(pend, first)
                rs, trs = rRS.next()
                TS("dve", rs[:, 0:n], den[:, 0:n], ESINK[:, la, j:j + 1], None, ALU.add, ALU.bypass, [tden, tC], [trs])
                RECIP(rs[:, 0:n], rs[:, 0:n], [trs], [trs])
                TT("dve", QO[:, j, s0:s0 + n], acc[:, 0:n], rs[:, 0:n], ALU.mult, [tacc, trs], [tQO[j][ti]])
        groups = [[0], [1], [2], [3]] + ([[4]] if ctx_out else [])
        out_proj(lambda j: d_aw[la, 12 + j], lambda c, ti: QO[:, c, TILES[ti][0]:TILES[ti][0] + TILES[ti][1]], lambda c, ti: tQO[c][ti],
                 groups, der, tder, 0, tFm, nslot=2)

    def mixer_c(l, b, der, tder, ctx_out):
        allt = [0, 1, 2, 3, 4]
        qt = allt if ctx_out else [0, 1, 2, 3]
        prenorm(allt, H, [[tH[c][ti] for ti in allt] for c in range(DC)], [TILES[ti][0] for ti in allt], der, tder, 0, l, 0, b, NB)

        def proj_norm(wsrc, tiles, dst, dc, tdrow, gcol):
            w, tw = wload(wsrc)
            for ti in tiles:
                s0, n = TILES[ti]
                ps, tps = rMAIN.next()
                mm(ps[:, 0:n], tps, [(w[:, c, :], H[:, c, s0:s0 + n]) for c in range(DC)], reads=[tw] + [tH[c][ti] for c in range(DC)])
                sq, tsq = rSQ.next()
                ACT(sq[:, 0:n], ps[:, 0:n], AF.Square, [tps], [tsq])
                ss, tss = rAUX.next()
                MM1(ss[:, 0:n], ones, sq[:, 0:n], True, True, [tsq, tC], [tss])
                rs, trs = rstd_from_ps(ss, tss, n, 1.0 / 128, 0)
                STT("dve", dst[:, dc, s0:s0 + n], ps[:, 0:n], CG[:, gcol:gcol + 1], rs[:, 0:n], ALU.mult, ALU.mult, [tps, trs, tC], [tdrow[ti]])

        for h in range(DC):
            proj_norm(d_cw[h], qt, QO, h, tQO[h], 0)
        for kvh in range(2):
            for gi in range(2):
                proj_norm(d_cw[8 + 2 * kvh + gi], allt, KT, gi, tKT[gi], 1)
            chunks = [(KT, gi, tKT[gi]) for gi in range(2)]
            if kvh == 0:
                chunks = [(QO, h, tQO[h]) for h in range(DC)] + chunks
            rope_pass(1, chunks)
            wl = [wload(d_cw[12 + 2 * kvh]), wload(d_cw[12 + 2 * kvh + 1])]
            v_proj(wl, allt, 256)
            for gi in range(2):
                g = 2 * kvh + gi
                for h in (2 * g, 2 * g + 1):
                    for ti in qt:
                        (acc, tacc), (den, tden) = rACC.next()
                        s0, n = TILES[ti]
                        kbs = list(range(18)) if ti < 4 else [16, 17]
                        pend = None

                        def pv(item, first, last=False):
                            pt, tpt, kb = item
                            def emit(e, pt=pt, kb=kb, first=first, n=n, gi=gi, last=last, acc=acc, den=den):
                                return [e.matmul(acc[:, 0:n], VV[:, kb, gi * 128:(gi + 1) * 128], pt[:, 0:n], start=first, stop=last),
                                        e.matmul(den[:, 0:n], ones, pt[:, 0:n], start=first, stop=last)]
                            P.op("pe", emit, reads=[tpt, tVV[kb], tC], writes=[tacc, tden])

                        first = True
                        for kb in kbs:
                            kti = min(kb // 4, 4)
                            ps, tps = rMAIN.next()
                            MM1(ps[:, 0:n], KT[:, gi, kb * 128:(kb + 1) * 128], QO[:, h, s0:s0 + n], True, True,
                                [tKT[gi][kti], tQO[h][ti]], [tps])
                            pt, tpt = rPT.next()
                            ACT(pt[:, 0:n], ps[:, 0:n], AF.Exp, [tps], [tpt], scale=EXP_C)
                            if pend is not None:
                                pv(pend, first)
                                first = False
                            pend = (pt, tpt, kb)
                        pv(pend, first, True)
                        rs, trs = rRS.next()
                        RECIP(rs[:, 0:n], den[:, 0:n], [tden], [trs])
                        TT("dve", QO[:, h, s0:s0 + n], acc[:, 0:n], rs[:, 0:n], ALU.mult, [tacc, trs], [tQO[h][ti]])
        groups = [[0], [1], [2], [3]] + ([[4]] if ctx_out else [])
        out_proj(lambda j: d_cw[16 + j], lambda c, ti: QO[:, c, TILES[ti][0]:TILES[ti][0] + TILES[ti][1]], lambda c, ti: tQO[c][ti],
                 groups, der, tder, 0, tFm, nslot=2)

    BuT = ubf(0, 4096).rearrange("p (c t) -> p c t", c=DC)
    tBu = toks("Bu", DC, 2)
    BH = ubf(4096, 8192).rearrange("p (c t) -> p c t", c=DC)
    tBH = toks("BH", DC, 2)
    Bv = ubf(8192, 12288).rearrange("p (b n) -> p b n", b=8)
    tBv = toks("Bv", 8)
    tFb = toks("Fb", DC, 2)
    _al = [t for r in tBH for t in r] + list(tBv)
    _fl = [t for r in tFb for t in r]
    for t in _al:
        t.alias = tuple(_fl)
    for t in _fl:
        t.alias = tuple(_al)
    BBC = U[:, 12288:15360].rearrange("p (k n) -> p k n", k=3)
    BBS = U[:, 15360:16384].rearrange("p (g n) -> p g n", g=8)
    WST = ubf(16384, 16896).rearrange("p (g n) -> p g n", g=8)
    tBC = Tok("bconst")

    def mixer_b(l, b, der, tder, ctx_out):
        DMA("act", BBS.rearrange("p g n -> p (g n)"), d_bws.rearrange("p g n -> p (g n)"), [], [tBC], "bc")
        CP("dve", WST.rearrange("p g n -> p (g n)"), BBS.rearrange("p g n -> p (g n)"), [tBC], [tBC])
        DMA("act", BBS.rearrange("p g n -> p (g n)"), d_bbs.rearrange("p g n -> p (g n)"), [tBC], [tBC], "bc")
        for k3 in range(3):
            DMA("act", BBC[:, k3, :], d_bbc[k3], [], [tBC], "bc")
        groups = [[0, 1], [2, 3]] + ([[4]] if ctx_out else [])
        for G in groups:
            offs = []
            o = 0
            for ti in G:
                offs.append(o)
                o += TILES[ti][1]
            ntok = o
            nblk = ntok // 128
            prenorm(G, BH, [[tBH[c][k] for k in range(len(G))] for c in range(DC)], offs, der, tder, 0, l, 0, b, NB)
            for s in range(DC):
                wv, twv = wload(d_bw[8 + s])
                for k, ti in enumerate(G):
                    n = TILES[ti][1]
                    o = offs[k]
                    nb_ = n // 128
                    ps, tps = rMAIN.next()
                    for bi in range(nb_):
                        mm(ps[:, bi * 128:(bi + 1) * 128], tps, [(BH[:, c, o + bi * 128:o + (bi + 1) * 128], wv[:, c, :]) for c in range(DC)],
                           reads=[twv] + [tBH[c][k] for c in range(DC)])
                    tmp, ttmp = rTMP.next()
                    b0 = o // 128
                    TT("dve", tmp[:, 0:n].rearrange("p (b n) -> p b n", b=nb_), ps[:, 0:n].rearrange("p (b n) -> p b n", b=nb_),
                       BBC[:, 0:1, s * 128:(s + 1) * 128].to_broadcast([128, nb_, 128]), ALU.add, [tps, tBC], [ttmp])
                    ACT(Bv[:, b0:b0 + nb_, s * 128:(s + 1) * 128], tmp[:, 0:n].rearrange("p (b n) -> p b n", b=nb_), AF.Gelu,
                        [ttmp], [tBv[b0 + i] for i in range(nb_)])
            for bi in range(nblk):
                st = LNS[:, bi, :]
                tst = tLNS[bi]
                junk, tjunk = rTMP.next()
                junk2, tjunk2 = rTMP.next()
                jb = junk[:, :].bitcast(BF16)
                jb2 = junk2[:, :].bitcast(BF16)
                ACT(jb, Bv[:, bi, :], AF.Copy, [tBv[bi]], [tjunk, tst], accum_out=st[:, 0:1])
                ACT(jb2, Bv[:, bi, :], AF.Square, [tBv[bi]], [tjunk2, tst], accum_out=st[:, 1:2])
                TS("dve", st[:, 2:3], st[:, 0:1], 1.0 / 1024, None, ALU.mult, ALU.bypass, [tst], [tst])
                TT("dve", st[:, 3:4], st[:, 2:3], st[:, 2:3], ALU.mult, [tst], [tst])
                STT("dve", st[:, 4:5], st[:, 1:2], 1.0 / 1024, st[:, 3:4], ALU.mult, ALU.subtract, [tst], [tst])
                ACT(st[:, 5:6], st[:, 4:5], AF.Sqrt, [tst, tC], [tst], scale=1.0, bias=EPS[:, 1:2])
                RECIP(st[:, 6:7], st[:, 5:6], [tst], [tst])
                for hh in range(2):
                    cs = slice(hh * 512, (hh + 1) * 512)
                    t1, tt1 = rTMP.next()
                    TS("dve", t1[:, :], Bv[:, bi, cs], st[:, 2:3], st[:, 6:7], ALU.subtract, ALU.mult, [tBv[bi], tst], [tt1])
                    TT("pool", t1[:, :], t1[:, :], BBC[:, 1, cs], ALU.mult, [tt1, tBC], [tt1])
                    TT("pool", Bv[:, bi, cs], t1[:, :], BBC[:, 2, cs], ALU.add, [tt1, tBC], [tBv[bi]])
            for j in range(DC):
                wu, twu = wload(d_bw[j])
                for k, ti in enumerate(G):
                    n = TILES[ti][1]
                    o = offs[k]
                    ps, tps = rMAIN.next()
                    mm(ps[:, 0:n], tps, [(wu[:, c, :], BH[:, c, o:o + n]) for c in range(DC)], reads=[twu] + [tBH[c][k] for c in range(DC)])
                    ACT(BuT[:, j, o:o + n], ps[:, 0:n], AF.Gelu, [tps, tC], [tBu[j][k]], bias=BBU[:, j:j + 1])
            for k, ti in enumerate(G):
                n = TILES[ti][1]
                o = offs[k]
                nb_ = n // 128
                b0 = o // 128
                for g in range(DC):
                    ps, tps = rMAIN.next()

                    def emit(e, ps=ps, g=g, b0=b0, nb_=nb_):
                        return [e.matmul(ps[:, bi * 128:(bi + 1) * 128], Bv[:, b0 + bi, g * 128:(g + 1) * 128], WST[:, g, :], start=True, stop=True)
                                for bi in range(nb_)]
                    P.op("pe", emit, reads=[tBv[b0 + i] for i in range(nb_)] + [tBC], writes=[tps])
                    tmp, ttmp = rTMP.next()
                    TT("dve", tmp[:, 0:n].rearrange("p (b n) -> p b n", b=nb_), ps[:, 0:n].rearrange("p (b n) -> p b n", b=nb_),
                       BBS[:, g:g + 1, :].to_broadcast([128, nb_, 128]), ALU.add, [tps, tBC], [ttmp])
                    TT("pool", BuT[:, g, o:o + n], tmp[:, 0:n], BuT[:, g, o:o + n], ALU.mult, [ttmp, tBu[g][k]], [tBu[g][k]])
            Gl = list(G)
            out_proj(lambda j: d_bw[16 + j], lambda c, ti, Gl=Gl, offs=offs: BuT[:, c, offs[Gl.index(ti)]:offs[Gl.index(ti)] + TILES[ti][1]],
                     lambda c, ti, Gl=Gl: tBu[c][Gl.index(ti)], [G], der, tder, 4096, tFb)

    def mixer(l, b, der, tder, last):
        kind = l % 3
        if kind == 0:
            mixer_a(l, b, der, tder, not last)
        elif kind == 1:
            mixer_b(l, b, der, tder, not last)
        else:
            mixer_c(l, b, der, tder, not last)

    setup()
    for b in range(NB):
        P.dma("act", lambda e, b=b: [e.dma_start(out=X[:, c, :], in_=d_xT[b, :, c, :]) for c in range(DC)],
              writes=[t for r in tX for t in r], key="xin", n=DC)
        for l in range(NL):
            last = (l == DEPTH - 1)
            der, tder = derive(l, b, l % 2)
            if do_mixer:
                P.barrier()
                mixer(l, b, der, tder, last)
            if do_ffn:
                P.barrier()
                ffn(l, b, der, tder, not last)
        for ti in range(4):
            s0, n = TILES[ti]
            DMA("act", d_out[b, :, :, s0:s0 + n], X[:, :, s0:s0 + n], [tX[c][ti] for c in range(DC)], [tOUT], "out")
    P.op("act", lambda e: e.nop(), reads=[tOUT])
    P.build(nc)
    es.close()
    return nc


def _slabs(W, cols_list):
    din = W.shape[0]
    C = din // 128
    out = np.empty((len(cols_list), 128, C, 128), np.float32)
    Wr = W.reshape(C, 128, W.shape[1])
    for s, cols in enumerate(cols_list):
        out[s] = Wr[:, :, cols].transpose(1, 0, 2)
    return out


def _rope_tables(head_dim):
    GRID_W = 64
    rows = NLAT // GRID_W
    row_pos = np.repeat(np.arange(rows, dtype=np.float32), GRID_W)
    col_pos = np.tile(np.arange(GRID_W, dtype=np.float32), rows)
    n_freq = head_dim // 4
    inv_freq = (10000.0 ** (-np.arange(n_freq, dtype=np.float32) / n_freq)).astype(np.float32)
    ang = np.concatenate([row_pos[:, None] * inv_freq, col_pos[:, None] * inv_freq], -1)
    half = head_dim // 2
    p = np.arange(128)
    idx = p % half
    sign = np.where((p % head_dim) < half, -1.0, 1.0).astype(np.float32)
    cos = np.cos(ang)[:, idx].T.astype(np.float32)
    sin = (np.sin(ang)[:, idx].T * sign[:, None]).astype(np.float32)
    return np.stack([cos, sin], 0)


def _perm(head_dim):
    half = head_dim // 2
    m = np.zeros((128, 128), np.float32)
    for mm_ in range(128):
        partner = mm_ + half if (mm_ % head_dim) < half else mm_ - half
        m[partner, mm_] = 1.0
    return m


def a_head_of(j, hf):
    gp, i = j // 4, j % 4
    return 8 * gp + 4 * hf + i


def prep_shared(inp):
    f = lambda k: np.asarray(inp[k], np.float32)
    sh = {}
    ada_w = f("ada_w")
    sh["adaw"] = np.stack([_slabs(ada_w[l], [np.arange(j * 128, (j + 1) * 128) for j in range(48)]) for l in range(DEPTH)], 0)
    sh["adab"] = np.ascontiguousarray(f("ada_b").reshape(DEPTH, 48, 128).transpose(2, 0, 1))
    sh["ng"] = np.ascontiguousarray(f("norm_g").reshape(DEPTH, 4, DC, 128).transpose(3, 0, 1, 2))
    win = f("ffn_w_in")
    cols = []
    for hc in range(HC):
        cols.append(np.arange(hc * 128, (hc + 1) * 128))
        cols.append(np.arange(FH + hc * 128, FH + (hc + 1) * 128))
    sh["fin"] = np.stack([_slabs(win[l], cols) for l in range(DEPTH)], 0)
    wout = f("ffn_w_out")
    fo = np.zeros((DEPTH, DC, 3, 128, 8, 128), np.float32)
    for l in range(DEPTH):
        Wr = wout[l].reshape(HC, 128, D)
        for j in range(DC):
            blk = Wr[:, :, j * 128:(j + 1) * 128].transpose(1, 0, 2)
            fo[l, j, 0] = blk[:, 0:8]
            fo[l, j, 1] = blk[:, 8:16]
            fo[l, j, 2, :, 0:6] = blk[:, 16:22]
    sh["fout"] = fo
    aq, ao, asink = f("a_w_qkv"), f("a_w_o"), f("a_sink")
    aw = np.zeros((2, 20, 128, DC, 128), np.float32)
    ask = np.zeros((128, 2, DC), np.float32)
    for la in range(2):
        qcols = []
        for j in range(DC):
            qcols.append(np.concatenate([np.arange(a_head_of(j, hf) * 64, a_head_of(j, hf) * 64 + 64) for hf in range(2)]))
        kcols = [np.arange(1024 + g * 128, 1024 + (g + 1) * 128) for g in range(2)]
        vcols = [np.arange(1280 + g * 128, 1280 + (g + 1) * 128) for g in range(2)]
        aw[la, 0:12] = _slabs(aq[la], qcols + kcols + vcols)
        rows = np.concatenate(qcols)
        wo_p = ao[la][rows, :]
        aw[la, 12:20] = _slabs(wo_p, [np.arange(j * 128, (j + 1) * 128) for j in range(DC)])
        for j in range(DC):
            for hf in range(2):
                ask[hf * 64:(hf + 1) * 64, la, j] = asink[la, a_head_of(j, hf)]
    sh["aw"] = aw
    sh["asink"] = ask
    bwin, bwo = f("b_w_in")[0], f("b_w_o")[0]
    sh["bw"] = np.concatenate([_slabs(bwin, [np.arange(j * 128, (j + 1) * 128) for j in range(16)]),
                               _slabs(bwo, [np.arange(j * 128, (j + 1) * 128) for j in range(8)])], 0)
    bbin = f("b_b_in")[0]
    sh["bbu"] = np.ascontiguousarray(bbin[:1024].reshape(DC, 128).T)
    sh["bbc"] = np.stack([np.broadcast_to(bbin[1024:], (128, 1024)), np.broadcast_to(f("b_ln_g")[0], (128, 1024)),
                          np.broadcast_to(f("b_ln_b")[0], (128, 1024))], 0).astype(np.float32)
    sh["bws"] = np.ascontiguousarray(f("b_w_s")[0].transpose(2, 0, 1))
    sh["bbs"] = np.ascontiguousarray(np.broadcast_to(f("b_b_s")[0][None], (128, 8, 128))).astype(np.float32)
    cq, co = f("c_w_qkv")[0], f("c_w_o")[0]
    sh["cw"] = np.concatenate([_slabs(cq, [np.arange(j * 128, (j + 1) * 128) for j in range(16)]),
                               _slabs(co, [np.arange(j * 128, (j + 1) * 128) for j in range(8)])], 0)
    sh["cg"] = np.stack([f("c_q_g")[0], f("c_k_g")[0]], 1).astype(np.float32)
    sh["rope"] = np.stack([_rope_tables(64), _rope_tables(128)], 0)
    ident = np.eye(128, dtype=np.float32)
    k = np.arange(128)[:, None]
    q = np.arange(128)[None, :]
    mnext = np.where(k <= q, 0.0, -30000.0).astype(np.float32)
    mprev = np.where(q <= k, 0.0, -30000.0).astype(np.float32)
    sh["cst"] = np.stack([_perm(64), _perm(128), ident, mnext, mprev], 0)
    return sh


def prep_core(inp, bs):
    x = np.asarray(inp["x"], np.float32)[bs]
    ctx = np.asarray(inp["ctx"], np.float32)[bs]
    nb = x.shape[0]
    xc = np.concatenate([x, ctx], 1)
    xT = np.ascontiguousarray(xc.reshape(nb, T, DC, 128).transpose(0, 3, 2, 1))
    c = np.asarray(inp["c"], np.float32)[bs]
    cc = np.concatenate([c, np.asarray(inp["c_ctx"], np.float32)[None]], 0)
    cT = np.ascontiguousarray(cc.reshape(nb + 1, DC, 128).transpose(2, 1, 0))
    return {"xT": xT, "cT": cT}


_NC_CACHE = {}


def kernel(**inputs):
    NB = 4
    key = ("full", NB)
    if key not in _NC_CACHE:
        _NC_CACHE[key] = build_program(NB=NB)
    nc = _NC_CACHE[key]
    sh = prep_shared(inputs)
    in_maps = []
    for i in range(NCORES):
        m = dict(sh)
        m.update(prep_core(inputs, slice(i * NB, (i + 1) * NB)))
        in_maps.append(m)
    res = run_bass_kernel_spmd(nc, in_maps, core_ids=list(range(NCORES)))
    outs = []
    for i in range(NCORES):
        oT = np.asarray(res.results[i]["outT"])
        outs.append(oT.transpose(0, 3, 2, 1).reshape(NB, NLAT, D))
    return np.ascontiguousarray(np.concatenate(outs, 0)).astype(np.float32)
```
